# Optimizing a Trainium2 kernel written in Bass

```python
import jax, jax.numpy as jnp
from jax import lax
import numpy as np

D_MODEL = 1024
BATCH = 4
SEQ = 4096
DEPTH = 2

CHUNK = 64
N_MIXERS = 2
N_POOL_LAYERS = (DEPTH + N_MIXERS - 1) // N_MIXERS
N_GLA_LAYERS = DEPTH // N_MIXERS

POOL_WIDTH = D_MODEL
POOL_WINDOWS = (2, 4, 8, 16)
POOL_GROUPS = len(POOL_WINDOWS)
POOL_GROUP_DIM = POOL_WIDTH // POOL_GROUPS

GLA_HEADS = 4
GLA_KEY_WIDTH = D_MODEL // 2
GLA_VALUE_WIDTH = D_MODEL
GLA_HEAD_K = GLA_KEY_WIDTH // GLA_HEADS
GLA_HEAD_V = GLA_VALUE_WIDTH // GLA_HEADS
GLA_GATE_RANK = 16
GLA_GATE_NORMALIZER = 16.0
GLA_IN_WIDTH = 2 * GLA_KEY_WIDTH + 2 * GLA_VALUE_WIDTH + GLA_GATE_RANK

RMS_EPS = 1e-6

kernel_name = "hybrid_pool_gla_streaming_trunk"


def rms_norm(x, w):
    xf = x.astype(jnp.float32)
    y = xf * lax.rsqrt(jnp.mean(xf * xf, axis=-1, keepdims=True) + RMS_EPS)
    return (y * w.astype(jnp.float32)).astype(x.dtype)


def trailing_mean(u, window):
    seq = u.shape[1]
    cs = jnp.cumsum(u.astype(jnp.float32), axis=1)
    lagged = jnp.pad(cs, ((0, 0), (window, 0), (0, 0)))[:, :seq]
    count = jnp.minimum(jnp.arange(1, seq + 1), window).astype(jnp.float32)
    return ((cs - lagged) / count[None, :, None]).astype(u.dtype)


def pool_mixer(h, in_w, group_w, group_b, scale, out_w):
    b, s, _ = h.shape
    u, gate = jnp.split(h @ in_w, 2, axis=-1)
    ug = u.reshape(b, s, POOL_GROUPS, POOL_GROUP_DIM)
    pooled = jnp.stack([trailing_mean(ug[:, :, g], w) for g, w in enumerate(POOL_WINDOWS)],
                       axis=2) - ug
    mixed = jnp.einsum('bsgc,gcd->bsgd', pooled, group_w) + group_b
    y = mixed.reshape(b, s, POOL_WIDTH) * scale * jax.nn.silu(gate)
    return y @ out_w


def gla_mixer(h, in_w, gk_w, gk_b, head_norm_w, out_w):
    b, s, _ = h.shape
    n_chunks = s // CHUNK
    f32 = jnp.float32
    proj = h @ in_w
    q, k, v, gate, gk_low = jnp.split(
        proj, [GLA_KEY_WIDTH, 2 * GLA_KEY_WIDTH, 2 * GLA_KEY_WIDTH + GLA_VALUE_WIDTH,
               2 * GLA_KEY_WIDTH + 2 * GLA_VALUE_WIDTH], axis=-1)
    log_g = jax.nn.log_sigmoid((gk_low @ gk_w + gk_b).astype(f32)) / GLA_GATE_NORMALIZER

    def to_chunks(t, d):
        return t.astype(f32).reshape(b, n_chunks, CHUNK, GLA_HEADS, d)

    qc = to_chunks(q, GLA_HEAD_K) * (GLA_HEAD_K ** -0.5)
    kc = to_chunks(k, GLA_HEAD_K)
    vc = to_chunks(v, GLA_HEAD_V)
    cum = jnp.cumsum(to_chunks(log_g, GLA_HEAD_K), axis=2)
    cum_last = cum[:, :, -1:]
    e_pos, e_neg = jnp.exp(cum), jnp.exp(-cum)

    fwd = jnp.einsum('bnthk,bnshk->bnhts', qc * e_pos, kc * e_neg)
    bwd = jnp.einsum('bnthk,bnshk->bnhts', qc * e_neg, kc * e_pos)
    idx = jnp.arange(CHUNK)
    lower = idx[:, None] >= idx[None, :]
    scores = jnp.where(lower, fwd, bwd)
    o_intra = jnp.einsum('bnhts,bnshv->bnthv', scores, vc)

    q_dec = qc * e_pos
    k_dec = kc * jnp.exp(cum_last - cum)
    chunk_decay = jnp.exp(cum_last[:, :, 0])

    def step(state, inp):
        q_n, k_n, v_n, d_n = inp
        o_n = jnp.einsum('bthk,bhkv->bthv', q_n, state)
        state = state * d_n[..., None] + jnp.einsum('bthk,bthv->bhkv', k_n, v_n)
        return state, o_n

    xs = (jnp.moveaxis(q_dec, 1, 0), jnp.moveaxis(k_dec, 1, 0),
          jnp.moveaxis(vc, 1, 0), jnp.moveaxis(chunk_decay, 1, 0))
    state0 = jnp.zeros((b, GLA_HEADS, GLA_HEAD_K, GLA_HEAD_V), f32)
    _, o_inter = lax.scan(step, state0, xs)
    o = (o_intra + jnp.moveaxis(o_inter, 0, 1)).reshape(b, s, GLA_HEADS, GLA_HEAD_V)

    o = o * lax.rsqrt(jnp.mean(o * o, axis=-1, keepdims=True) + RMS_EPS) * head_norm_w.astype(f32)
    y = o.reshape(b, s, GLA_VALUE_WIDTH).astype(h.dtype) * jax.nn.silu(gate)
    return y @ out_w


def setup_inputs(seed: int = 0) -> dict:
    key = jax.random.key(seed)
    ks = jax.random.split(key, 16)
    nrm = jax.random.normal
    f32 = jnp.float32
    NP, NG = N_POOL_LAYERS, N_GLA_LAYERS
    return {
        "x": nrm(ks[0], (BATCH, SEQ, D_MODEL), f32),
        "norm_w": 1.0 + 0.02 * nrm(ks[1], (DEPTH, D_MODEL), f32),
        "pool_in_w": nrm(ks[2], (NP, D_MODEL, 2 * POOL_WIDTH), f32) * D_MODEL ** -0.5,
        "pool_group_w": nrm(ks[3], (NP, POOL_GROUPS, POOL_GROUP_DIM, POOL_GROUP_DIM), f32) * POOL_GROUP_DIM ** -0.5,
        "pool_group_b": 0.02 * nrm(ks[4], (NP, POOL_GROUPS, POOL_GROUP_DIM), f32),
        "pool_scale": 1.0 + 0.1 * nrm(ks[5], (NP, POOL_WIDTH), f32),
        "pool_out_w": nrm(ks[6], (NP, POOL_WIDTH, D_MODEL), f32) * POOL_WIDTH ** -0.5,
        "gla_in_w": nrm(ks[7], (NG, D_MODEL, GLA_IN_WIDTH), f32) * D_MODEL ** -0.5,
        "gla_gk_w": nrm(ks[8], (NG, GLA_GATE_RANK, GLA_KEY_WIDTH), f32) * GLA_GATE_RANK ** -0.5,
        "gla_gk_b": 0.02 * nrm(ks[9], (NG, GLA_KEY_WIDTH), f32),
        "gla_head_norm_w": 1.0 + 0.02 * nrm(ks[10], (NG, GLA_HEAD_V), f32),
        "gla_out_w": nrm(ks[11], (NG, GLA_VALUE_WIDTH, D_MODEL), f32) * GLA_VALUE_WIDTH ** -0.5,
        "final_norm_w": 1.0 + 0.02 * nrm(ks[12], (D_MODEL,), f32),
    }


def reference(x, norm_w, pool_in_w, pool_group_w, pool_group_b, pool_scale, pool_out_w,
              gla_in_w, gla_gk_w, gla_gk_b, gla_head_norm_w, gla_out_w, final_norm_w):
    h = x
    for i in range(DEPTH):
        normed = rms_norm(h, norm_w[i])
        j = i // N_MIXERS
        if i % N_MIXERS == 0:
            h = h + pool_mixer(normed, pool_in_w[j], pool_group_w[j], pool_group_b[j],
                               pool_scale[j], pool_out_w[j])
        else:
            h = h + gla_mixer(normed, gla_in_w[j], gla_gk_w[j], gla_gk_b[j],
                              gla_head_norm_w[j], gla_out_w[j])
    return rms_norm(h, final_norm_w)
```

```python
from contextlib import ExitStack

import numpy as np

import concourse.bass as bass
import concourse.mybir as mybir
from concourse.bass_utils import run_bass_kernel_spmd

F32 = mybir.dt.float32
BF16 = mybir.dt.bfloat16
AF = mybir.ActivationFunctionType
ALU = mybir.AluOpType

D = 1024
NT = 16
TOK = NT * 128
EPS = 1e-6
QSCALE = 128 ** -0.5


class Buf:
    __slots__ = ("name", "last_w", "readers", "sem")

    def __init__(self, name):
        self.name = name
        self.last_w = None
        self.readers = []
        self.sem = None


class DmaGroup:
    def __init__(self, name):
        self.name = name
        self.sem = None
        self.total = 0


class Op:
    __slots__ = ("eng", "fn", "deps", "signal", "val", "sem", "is_dma", "buf", "group", "inc")

    def __init__(self, eng, fn):
        self.eng = eng
        self.fn = fn
        self.deps = []
        self.signal = False
        self.val = None
        self.sem = None
        self.is_dma = False
        self.buf = None
        self.group = None


class Sched:
    ENGS = ("pe", "act", "dve", "pool", "sp")

    def __init__(self, nc, stack):
        self.nc = nc
        self.stack = stack
        self.ops = {e: [] for e in self.ENGS}
        self.engobj = {"pe": nc.tensor, "act": nc.scalar, "dve": nc.vector,
                       "pool": nc.gpsimd, "sp": nc.sync}
        self.all_ops = []

    def _track(self, op, reads, writes):
        deps = op.deps
        for b in reads:
            if b.last_w is not None:
                deps.append(b.last_w)
        for b in writes:
            if b.last_w is not None:
                deps.append(b.last_w)
            deps.extend(b.readers)
        for b in reads:
            b.readers.append(op)
        for b in writes:
            b.last_w = op
            b.readers = []
        self.ops[op.eng].append(op)
        self.all_ops.append(op)

    def op(self, eng, fn, reads=(), writes=()):
        o = Op(eng, fn)
        self._track(o, reads, writes)
        return o

    def dma(self, eng, fn, reads=(), writes=(), buf=None, group=None, inc=16):
        o = Op(eng, fn)
        o.inc = inc
        o.is_dma = True
        o.buf = buf
        o.group = group
        self._track(o, reads, writes)
        return o

    def lower(self):
        nc = self.nc
        for o in self.all_ops:
            for d in o.deps:
                if d.is_dma:
                    continue
                if d.eng == "pe" and o.eng == "pe" and not o.is_dma:
                    continue
                d.signal = True
        esem = {e: self.stack.enter_context(nc.semaphore("sem_" + e)) for e in self.ENGS}
        cnt = {e: 0 for e in esem}
        dma_cnt = {}
        for o in self.all_ops:
            if o.is_dma:
                if o.group is not None:
                    g = o.group
                    if g.sem is None:
                        g.sem = self.stack.enter_context(nc.semaphore("g_" + g.name))
                    g.total += o.inc
                    o.sem = g.sem
                else:
                    b = o.buf
                    if b.sem is None:
                        b.sem = self.stack.enter_context(nc.semaphore("b_" + b.name))
                        dma_cnt[b] = 0
                    dma_cnt[b] += o.inc
                    o.sem = b.sem
                    o.val = dma_cnt[b]
            else:
                o.sem = esem[o.eng]
                if o.signal:
                    cnt[o.eng] += 1
                    o.val = cnt[o.eng]
        for o in self.all_ops:
            if o.is_dma and o.group is not None:
                o.val = o.group.total
        final = {}
        for e, lst in self.ops.items():
            eng = self.engobj[e]
            waited = {}
            for o in lst:
                need = {}
                for d in o.deps:
                    if (not d.is_dma) and d.eng == "pe" and e == "pe" and not o.is_dma:
                        continue
                    if d.is_dma and o.is_dma and d.group is not None and d.group is o.group:
                        continue
                    if need.get(d.sem, (0, None))[0] < d.val:
                        need[d.sem] = (d.val, d.sem)
                for key, (v, s) in need.items():
                    if waited.get(key, 0) >= v:
                        continue
                    eng.wait_ge(s, v)
                    waited[key] = v
                ins = o.fn()
                if o.is_dma:
                    ins.then_inc(o.sem, o.inc)
                    final[o.sem] = (o.sem, o.val)
                elif o.signal:
                    ins.then_inc(o.sem, 1)
        eng = self.engobj["sp"]
        for s, v in final.values():
            eng.wait_ge(s, v)


class Rot:
    def __init__(self, items):
        self.items = items
        self.i = 0

    def next(self):
        it = self.items[self.i % len(self.items)]
        self.i += 1
        return it


def build(mode="F"):
    nc = bass.Bass("TRN2", target_bir_lowering=False)
    dt_in = lambda n, s: nc.dram_tensor(n, s, F32, kind="ExternalInput").ap()
    dt_out = lambda n, s: nc.dram_tensor(n, s, F32, kind="ExternalOutput").ap()

    d_ident = dt_in("ident", [128, 128])
    d_normw = dt_in("norm_w", [2, D])
    d_x = dt_in("x", [128 + TOK, D])
    d_inw = dt_in("pool_in_w", [D, 2 * D])
    d_gw = dt_in("pool_group_w", [4, 256, 256])
    d_gb = dt_in("pool_group_b", [D])
    d_sc = dt_in("pool_scale", [D])
    d_ow = dt_in("pool_out_w", [D, D])
    d_pcur = dt_in("pcur", [128, 4, 128])
    d_pcurf = dt_in("pcurf", [128, 4, 128])
    d_maskA = dt_in("mask_a", [128, 1])
    d_maskB = dt_in("mask_b", [128, 1])
    d_pprev = dt_in("pprev", [128, 4, 128])
    d_gin = dt_in("gla_in_w", [D, 3088])
    d_gkw = dt_in("gla_gk_w", [16, 512])
    d_gkb = dt_in("gla_gk_b", [1, 512])
    d_ones = dt_in("ones", [1, 128])
    d_tri = dt_in("tri", [128, 128])
    d_trirev = dt_in("trirev", [128, 128])
    d_tot = dt_in("tot", [128, 2])
    d_hnw = dt_in("hnw2", [128, 2])
    d_gow = dt_in("gla_out_w", [D, D])
    d_fnw = dt_in("final_norm_w", [D])
    d_mf = dt_in("mf", [128, 4, 128])
    d_mb = dt_in("mb", [128, 4, 128])
    d_out = dt_out("out", [TOK, D])
    d_xin_t = nc.dram_tensor("xchg_in", [128, 1024], F32)
    d_xout_t = nc.dram_tensor("xchg_out", [128, 1024], F32)

    with ExitStack() as st:
        S = Sched(nc, st)

        def sb(name, shape, dt):
            return st.enter_context(nc.sbuf_tensor(name, shape, dt)), Buf(name)

        def sbn(name, shape, dt, n):
            return Rot([sb("%s%d" % (name, i), shape, dt) for i in range(n)])

        def apn(rot):
            return Rot([(t_[:], b_) for (t_, b_) in rot.items])

        _ps8 = [(st.enter_context(nc.psum_tensor("psf%d" % i, [128, 512], F32)), Buf("psf%d" % i))
                for i in range(8)]
        psf = Rot(_ps8)
        psg = Rot(_ps8[0:6])
        pso = Rot(_ps8[6:8])

        def v3(ap, a):
            return ap.rearrange("p (a b) -> p a b", a=a)

        h1, _ = sb("h1sb", [128, NT, D], F32)
        b_h1 = [Buf("h1_%d" % i) for i in range(NT)]
        RA, _ = sb("RA", [128, 26624], BF16)
        w_in0 = RA[:, 0:16384].rearrange("p (k n) -> p k n", k=8)
        gw = RA[:, 16384:18432].rearrange("p (g i n) -> p g i n", g=4, i=2)
        w_out0 = RA[:, 18432:26624].rearrange("p (k n) -> p k n", k=8)
        b_win0u, b_win0g, b_gw, b_wout0 = Buf("w_in0u"), Buf("w_in0g"), Buf("gw"), Buf("w_out0")
        wgA = RA[:, 0:12288].rearrange("p (k n) -> p k n", k=8)
        b_wgA = Buf("wgA")
        qk3_r = Rot([(RA[:, 12288 + j * 1536:12288 + (j + 1) * 1536].rearrange("p (v h t) -> p v h t", v=3, h=4),
                      Buf("qk3_%d" % j)) for j in range(2)])
        qp_ra = [(RA[:, 15360 + j * 512:15360 + (j + 1) * 512].rearrange("p (h t) -> p h t", h=4), Buf("qpos%d" % j))
                 for j in range(2)]
        t12, b_t12 = RA[:, 16384:17408].rearrange("p (v h t) -> p v h t", v=2, h=4), Buf("t12")
        sc_r = Rot([(RA[:, 17408 + j * 512:17408 + (j + 1) * 512].rearrange("p (h t) -> p h t", h=4),
                     Buf("sc%d" % j)) for j in range(2)])
        w_out1 = RA[:, 18432:26624].rearrange("p (k n) -> p k n", k=8)
        b_wout1 = Buf("w_out1")
        wgB, b_wgB = sb("wgB", [128, 8, 1552], BF16)
        nwbc0, b_nwbc0 = sb("nwbc0", [128, D], F32)
        nwbc1, b_nwbc1 = sb("nwbc1", [128, D], BF16)
        ident, b_ident = sb("identb", [128, 128], BF16)
        gkwb, b_gkwb = sb("gkwb", [128, 512], BF16)
        tri, b_tri = sb("trib", [128, 128], BF16)
        trirev, b_trirev = sb("trirevb", [128, 128], BF16)
        tot, b_tot = sb("totb", [128, 2], BF16)
        pcur, b_pcur = sb("pcurb", [128, 4, 128], BF16)
        pcurf, b_pcurf = sb("pcurfb", [128, 4, 128], BF16)
        pprev, b_pprev = sb("pprevb", [128, 4, 128], BF16)
        maskA, b_maskA = sb("maskA", [128, 1], F32)
        maskB, b_maskB = sb("maskB", [128, 1], F32)
        gb, b_gb = sb("gb", [128, 8], F32)
        scl, b_scl = sb("scl", [128, 8], F32)
        hnw2, b_hnw2 = sb("hnw2sb", [128, 2], F32)
        Sst, b_S = sb("Sst", [128, 4, 256], F32)
        b_Sh = [Buf("S_h%d" % h_) for h_ in range(4)]
        Sbf, b_Sbf = sb("Sbf", [128, 4, 256], BF16)
        junk, _ = sb("junk", [128, D], BF16)
        fdummy, _ = sb("fdummy", [128, 16], F32)
        fcount = [0]
        stats = sbn("stats", [128, 8], F32, 8)
        xt_r = sbn("xt", [128, D], F32, 3)
        xn_r = sbn("xn", [128, D], BF16, 1)
        xnT_r = sbn("xnT", [128, 8, 128], BF16, 3)
        u_r = sbn("utok", [128, D], BF16, 3)
        sg_r = sbn("sg", [128, 8, 128], BF16, 3)
        pl_r = sbn("pooledT", [128, 8, 128], BF16, 2)
        yT_r = sbn("yT", [128, 8, 128], BF16, 2)
        kdT_r = apn(sbn("kdT", [128, 4, 128], BF16, 2))
        kdtok_r = apn(sbn("kdtok", [128, 4, 128], BF16, 2))
        lg_r = apn(sbn("lg", [128, 512], BF16, 1))
        qp3, b_qp3 = sb("qpos2", [128, 4, 128], BF16)
        qpos_r = Rot(qp_ra + [(qp3[:], b_qp3)])
        gkT_r = sbn("gkT", [128, 128], BF16, 2)
        dfac_r = sbn("dfac", [128, 4, 2], F32, 4)
        vtok_r = apn(u_r)
        sgt_r = Rot([(t_[:].rearrange("p a b -> p (a b)"), b_) for (t_, b_) in sg_r.items])
        y_r = Rot([(t_[:].rearrange("p a b -> p (a b)"), b_) for (t_, b_) in pl_r.items])
        xi = xt_r.items
        xsub = {}
        for j in range(3):
            for hlf in range(2):
                xsub[(j, hlf)] = (xi[j][0][:, hlf * 512:(hlf + 1) * 512].rearrange("p (a b) -> p a b", a=4),
                                  Buf("xt%d_%d" % (j, hlf)))
        epos_r = Rot([xsub[(0, 0)], xsub[(2, 0)]])
        eneg_r = Rot([xsub[(0, 1)], xsub[(2, 1)]])
        kfac_r = Rot([xsub[(1, 0)], xsub[(1, 1)]])
        xt_all_bufs = [b_ for (_, b_) in xi] + [b_ for (_, b_) in xsub.values()]

        def fence(bufs):
            k = fcount[0]
            fcount[0] += 1
            S.op("pool", lambda: nc.gpsimd.memset(fdummy[:, k:k + 1], 0.0), writes=bufs)

        gconst_d = {"pool": DmaGroup("const_pool"), "sp": DmaGroup("const_sp")}
        glate_d = {"pool": DmaGroup("late_pool"), "sp": DmaGroup("late_sp")}

        def cdma(out_ap, in_ap, b, eng="pool", slow=False, grp=None):
            e = nc.gpsimd if eng == "pool" else nc.sync
            g = grp if grp is not None else gconst_d[eng]
            if slow:
                S.dma(eng, lambda: e.dma_start(out=out_ap, in_=in_ap, allow_slow_non_contiguous=True),
                      writes=[b], group=g)
            else:
                S.dma(eng, lambda: e.dma_start(out=out_ap, in_=in_ap), writes=[b], group=g)

        cdma(ident[:], d_ident, b_ident)
        cdma(nwbc0[:], d_normw[0].partition_broadcast(128), b_nwbc0, "sp")

        def load_nwbc1():
            S.dma("sp", lambda: nc.sync.dma_start(out=h1[:, 15, :], in_=d_normw[1].partition_broadcast(128)),
                  writes=[b_h1[15]], buf=b_h1[15])
            S.op("pool", lambda: nc.gpsimd.tensor_copy(nwbc1[:], h1[:, 15, :]), reads=[b_h1[15]], writes=[b_nwbc1])
        cdma(maskA[:], d_maskA, b_maskA, "sp")
        cdma(maskB[:], d_maskB, b_maskB, "sp")
        cdma(gb[:], d_gb.rearrange("(k p) -> p k", p=128), b_gb, "sp", True)
        cdma(scl[:], d_sc.rearrange("(k p) -> p k", p=128), b_scl, "sp", True)
        gw0u, gw0g = DmaGroup("w0u"), DmaGroup("w0g")
        inw_v = d_inw.rearrange("(k p) n -> p k n", p=128)
        for k in range(8):
            S.dma("pool", lambda k=k: nc.gpsimd.dma_start(out=w_in0[:, k, 0:D], in_=inw_v[:, k, 0:D]),
                  writes=[b_win0u], group=gw0u)
        for k in range(8):
            S.dma("pool", lambda k=k: nc.gpsimd.dma_start(out=w_in0[:, k, D:2 * D], in_=inw_v[:, k, D:2 * D]),
                  writes=[b_win0g], group=gw0g)
        S.dma("pool", lambda: nc.gpsimd.dma_start(out=gw, in_=d_gw.rearrange("g (i p) n -> p g i n", p=128)),
              writes=[b_gw], group=gw0g)
        cdma(pcur[:], d_pcur, b_pcur)
        cdma(pcurf[:], d_pcurf, b_pcurf)
        cdma(pprev[:], d_pprev, b_pprev)
        ow_v = d_ow.rearrange("(k p) n -> p k n", p=128)

        gstg = DmaGroup("stg_wout0")

        S.dma("sp", lambda: nc.sync.dma_start(out=h1[:, 14, :], in_=d_sc.partition_broadcast(128)),
              writes=[b_h1[14]], buf=b_h1[14])
        sbc3 = h1[:, 14, :].rearrange("p (g n) -> p g n", g=4)

        def fold_pool_scale():
            S.op("dve", lambda: nc.vector.tensor_tensor(gb[:], gb[:], scl[:], ALU.mult),
                 reads=[b_gb, b_scl], writes=[b_gb])
            for ic in range(2):
                S.op("dve", lambda ic=ic: nc.vector.tensor_tensor(gw[:, :, ic, :], gw[:, :, ic, :], sbc3, ALU.mult),
                     reads=[b_gw, b_h1[14]], writes=[b_gw])

        def stage_wout0_dma(after_buf):
            for k in range(8):
                S.dma("pool", lambda k=k: nc.gpsimd.dma_start(out=w_out0[:, k, :], in_=ow_v[:, k, :]),
                      reads=[after_buf], writes=[b_wout0], group=gstg)

        def stage_wout0_scale():
            for k in range(8):
                if k % 2 == 0:
                    S.op("act", lambda k=k: nc.scalar.activation(w_out0[:, k, :], h1[:, 8 + k, :], AF.Copy,
                                                                 scale=scl[:, k:k + 1]),
                         reads=[b_h1[8 + k], b_scl], writes=[b_wout0])
                else:
                    S.op("dve", lambda k=k: nc.vector.tensor_scalar(w_out0[:, k, :], h1[:, 8 + k, :],
                                                                    scl[:, k:k + 1], None, ALU.mult),
                         reads=[b_h1[8 + k], b_scl], writes=[b_wout0])
        S.op("pool", lambda: nc.gpsimd.memset(gkwb[:], 0.0), writes=[b_gkwb])
        for (g_, bg_) in gkT_r.items:
            S.op("pool", lambda g_=g_: nc.gpsimd.memset(g_[:], 0.0), writes=[bg_])
        cdma(gkwb[0:16, :], d_gkw, b_gkwb)
        cdma(gkwb[16:17, :], d_gkb, b_gkwb)
        for (g_, bg_) in gkT_r.items:
            cdma(g_[16:17, :], d_ones, bg_)
        cdma(tri[:], d_tri, b_tri)
        cdma(trirev[:], d_trirev, b_trirev)
        cdma(tot[:], d_tot, b_tot)
        cdma(hnw2[:], d_hnw, b_hnw2, "sp")
        gwB = DmaGroup("wgB")
        gin_v = d_gin.rearrange("(k p) n -> p k n", p=128)
        gow_v = d_gow.rearrange("(k p) n -> p k n", p=128)

        def load_wgB():
            for k in range(8):
                S.dma("pool", lambda k=k: nc.gpsimd.dma_start(out=wgB[:, k, 0:1536], in_=gin_v[:, k, 512:2048]),
                      writes=[b_wgB], group=gwB)
            S.dma("pool", lambda: nc.gpsimd.dma_start(out=wgB[:, :, 1536:1552], in_=gin_v[:, :, 3072:3088]),
                  writes=[b_wgB], group=gwB)

        PS = {"g": psf}

        def mm(out_ap, lhsT, rhs, start, stop, reads, writes):
            S.op("pe", lambda: nc.tensor.matmul(out_ap, lhsT=lhsT, rhs=rhs, start=start, stop=stop),
                 reads=reads, writes=writes)

        def norm_a(t, src_ap, b_src, nwbc, b_nw):
            stt, b_st = stats.next()
            S.op("act", lambda: nc.scalar.activation(junk[:], src_ap, AF.Square, accum_out=stt[:, 0:1]),
                 reads=[b_src], writes=[b_st])
            S.op("act", lambda: nc.scalar.activation(stt[:, 1:2], stt[:, 0:1], AF.Ln, bias=EPS, scale=1.0 / D),
                 reads=[b_st], writes=[b_st])
            S.op("act", lambda: nc.scalar.activation(stt[:, 2:3], stt[:, 1:2], AF.Exp, scale=-0.5),
                 reads=[b_st], writes=[b_st])
            t["nrm"] = (stt, b_st, src_ap, b_src, nwbc, b_nw)

        def norm_d(t):
            stt, b_st, src_ap, b_src, nwbc, b_nw = t["nrm"]
            xn, b_xn = xn_r.next()
            S.op("dve", lambda: nc.vector.scalar_tensor_tensor(xn[:], src_ap, stt[:, 2:3], nwbc[:], ALU.mult, ALU.mult),
                 reads=[b_src, b_st, b_nw], writes=[b_xn])
            t["xn"], t["b_xn"] = xn, b_xn

        def norm_t(t):
            xn, b_xn = t["xn"], t["b_xn"]
            xnT, b_xnT = xnT_r.next()
            for hb in range(2):
                pT, b_pT = PS["g"].next()
                pT3 = v3(pT[:], 4)
                for c4 in range(4):
                    c = hb * 4 + c4
                    mm(pT3[:, c4, :], xn[:, c * 128:(c + 1) * 128], ident[:], True, True, [b_xn, b_ident], [b_pT])
                S.op("dve", lambda pT3=pT3, hb=hb: nc.vector.tensor_copy(xnT[:, hb * 4:(hb + 1) * 4, :], pT3),
                     reads=[b_pT], writes=[b_xnT])
            t["xnT"], t["b_xnT"] = xnT, b_xnT

        def run_pipe(stages, lo, hi):
            maxlag = max(l for _, l in stages)
            for step in range(lo, hi + maxlag):
                for fn, lag in stages:
                    i = step - lag
                    if lo <= i < hi:
                        fn(i)

        tiles = {}
        P1CFG = {"base": 0}

        def p1_load(i):
            if i == 9:
                load_wgB()
                load_nwbc1()
            xt, b_xt = xt_r.next()
            S.dma("sp", lambda: nc.sync.dma_start(out=xt[:], in_=d_x[i * 128:(i + 1) * 128, :]),
                  writes=[b_xt], buf=b_xt)
            tiles[i] = dict(xt=xt, b_xt=b_xt)

        def p1_norm_a(i):
            t = tiles[i]
            norm_a(t, t["xt"][:], t["b_xt"], nwbc0, b_nwbc0)
            ti = i - 1 - P1CFG["base"]
            if ti >= 0:
                S.op("pool", lambda: nc.gpsimd.tensor_copy(h1[:, ti, :], t["xt"][:]),
                     reads=[t["b_xt"]], writes=[b_h1[ti]])

        def p1_norm_t(i):
            norm_t(tiles[i])

        def p1_proj(i):
            t = tiles[i]
            xnT, b_xnT = t["xnT"], t["b_xnT"]
            u, b_u = u_r.next()
            t["u"], t["b_u"] = u, b_u
            for nb in range(2):
                pu, b_pu = psf.next()
                for k in range(8):
                    mm(pu[:], xnT[:, k, :], w_in0[:, k, nb * 512:(nb + 1) * 512], k == 0, k == 7,
                       [b_xnT, b_win0u], [b_pu])
                S.op("act", lambda pu=pu, nb=nb: nc.scalar.copy(u[:, nb * 512:(nb + 1) * 512], pu[:]),
                     reads=[b_pu], writes=[b_u])
            if i == P1CFG["base"]:
                fold_pool_scale()
                stage_wout0_dma(b_u)
                return
            sg, b_sg = sg_r.next()
            t["sg"], t["b_sg"] = sg, b_sg
            for hb in range(2):
                pg, b_pg = psf.next()
                pg3 = v3(pg[:], 4)
                for c4 in range(4):
                    cc = hb * 4 + c4
                    for k in range(8):
                        mm(pg3[:, c4, :], w_in0[:, k, D + cc * 128:D + (cc + 1) * 128], xnT[:, k, :],
                           k == 0, k == 7, [b_xnT, b_win0g], [b_pg])
                S.op("act", lambda pg3=pg3, hb=hb: nc.scalar.activation(sg[:, hb * 4:(hb + 1) * 4, :], pg3, AF.Silu),
                     reads=[b_pg], writes=[b_sg])

        def p1_pool(i):
            t = tiles[i]
            u, b_u = t["u"], t["b_u"]
            up, b_up = tiles[i - 1]["u"], tiles[i - 1]["b_u"]
            pl, b_pl = pl_r.next()
            t["pl"], t["b_pl"] = pl, b_pl
            pc_t, b_pc = (pcurf, b_pcurf) if i == P1CFG["base"] + 1 else (pcur, b_pcur)
            for hb in range(2):
                pp, b_pp = psf.next()
                pp3 = v3(pp[:], 4)
                for c4 in range(4):
                    cc = hb * 4 + c4
                    g = cc // 2
                    mm(pp3[:, c4, :], u[:, cc * 128:(cc + 1) * 128], pc_t[:, g, :], True, False,
                       [b_u, b_pc], [b_pp])
                    mm(pp3[:, c4, 0:16], up[:, cc * 128:(cc + 1) * 128], pprev[:, g, 0:16], False, True,
                       [b_up, b_pprev], [b_pp])
                S.op("dve", lambda pp3=pp3, hb=hb: nc.vector.tensor_copy(pl[:, hb * 4:(hb + 1) * 4, :], pp3),
                     reads=[b_pp], writes=[b_pl])

        def p1_group(i):
            t = tiles[i]
            pl, b_pl, sg, b_sg = t["pl"], t["b_pl"], t["sg"], t["b_sg"]
            yT, b_yT = yT_r.next()
            t["yT"], t["b_yT"] = yT, b_yT
            for hb in range(2):
                pm, b_pm = psf.next()
                pm3 = v3(pm[:], 4)
                for c4 in range(4):
                    oc = hb * 4 + c4
                    g, j = oc // 2, oc % 2
                    for ic in range(2):
                        mm(pm3[:, c4, :], gw[:, g, ic, j * 128:(j + 1) * 128], pl[:, 2 * g + ic, :],
                           ic == 0, ic == 1, [b_gw, b_pl], [b_pm])
                for c4 in range(4):
                    oc = hb * 4 + c4
                    S.op("dve", lambda pm3=pm3, c4=c4, oc=oc: nc.vector.scalar_tensor_tensor(
                        yT[:, oc, :], pm3[:, c4, :], gb[:, oc:oc + 1], sg[:, oc, :], ALU.add, ALU.mult),
                        reads=[b_pm, b_gb, b_sg], writes=[b_yT])

        def p1_out(i):
            t = tiles[i]
            yT, b_yT = t["yT"], t["b_yT"]
            ti = i - 1 - P1CFG["base"]
            for nb in range(2):
                po, b_po = psf.next()
                for k in range(8):
                    mm(po[:], yT[:, k, :], w_out0[:, k, nb * 512:(nb + 1) * 512], k == 0, k == 7,
                       [b_yT, b_wout0], [b_po])
                S.op("dve", lambda po=po, nb=nb: nc.vector.tensor_tensor(
                    h1[:, ti, nb * 512:(nb + 1) * 512], po[:], h1[:, ti, nb * 512:(nb + 1) * 512], ALU.add),
                    reads=[b_po, b_h1[ti]], writes=[b_h1[ti]])
            del tiles[i - 1]

        def p1_stages():
            P1CFG["base"] = 0
            own = lambda f: (lambda i: f(i) if i > 0 else None)
            return [(p1_norm_t, 2), (p1_norm_a, 1), (p1_proj, 3), (own(p1_pool), 4),
                    (lambda i: norm_d(tiles[i]), 1),
                    (own(p1_group), 5), (own(p1_out), 6), (p1_load, 0)]

        def run_multi(pipes, hooks):
            last = max(off + hi - 1 + max(l for _, l in st_) for (st_, lo, hi, off) in pipes)
            last = max(last, max(hooks) if hooks else 0)
            for step in range(0, last + 1):
                if step in hooks:
                    hooks[step]()
                for (st_, lo, hi, off) in pipes:
                    for fn, lag in st_:
                        i = step - off - lag
                        if lo <= i < hi:
                            fn(i)

        def gla_gk_a(t):
            xnT, b_xnT = t["xnT"], t["b_xnT"]
            pgk, b_pgk = PS["g"].next()
            for k in range(8):
                mm(pgk[0:16, 0:128], wgB[:, k, 1536:1552], xnT[:, k, :], k == 0, k == 7,
                   [b_xnT, b_wgB], [b_pgk])
            gkT, b_gkT = gkT_r.next()
            S.op("act", lambda: nc.scalar.copy(gkT[0:16, :], pgk[0:16, 0:128]), reads=[b_pgk], writes=[b_gkT])
            t["gkT"], t["b_gkT"] = gkT, b_gkT

        def gla_gk_b(t):
            gkT, b_gkT = t["gkT"], t["b_gkT"]
            pz, b_pz = PS["g"].next()
            mm(pz[:], gkT[:], gkwb[:], True, True, [b_gkT, b_gkwb], [b_pz])
            lg, b_lg = lg_r.next()
            S.op("act", lambda: nc.scalar.activation(pz[:], pz[:], AF.Exp, scale=-1.0), reads=[b_pz], writes=[b_pz])
            S.op("act", lambda: nc.scalar.activation(lg, pz[:], AF.Ln, bias=1.0), reads=[b_pz], writes=[b_lg])
            t["lg"], t["b_lg"] = lg, b_lg

        def gla_gk_c(t, with_cum):
            lg, b_lg = t["lg"], t["b_lg"]
            prc, b_prc = PS["g"].next()
            prc3 = v3(prc[:], 4)
            for h in range(4):
                mm(prc3[:, h, :], lg[:, h * 128:(h + 1) * 128], trirev[:], True, True, [b_lg, b_trirev], [b_prc])
            kfac, b_kfac = kfac_r.next()
            S.op("act", lambda: nc.scalar.activation(kfac, prc3, AF.Exp), reads=[b_prc], writes=[b_kfac])
            pd, b_pd = PS["g"].next()
            pd3 = pd[:, 0:8].rearrange("p (a b) -> p a b", a=4)
            for h in range(4):
                mm(pd3[:, h, :], lg[:, h * 128:(h + 1) * 128], tot[:], True, True, [b_lg, b_tot], [b_pd])
            dfac, b_dfac = dfac_r.next()
            S.op("act", lambda: nc.scalar.activation(dfac[:], pd3, AF.Exp), reads=[b_pd], writes=[b_dfac])
            t["kfac"], t["b_kfac"], t["dfac"], t["b_dfac"] = kfac, b_kfac, dfac, b_dfac
            if with_cum:
                pc, b_pc = PS["g"].next()
                pc3 = v3(pc[:], 4)
                for h in range(4):
                    mm(pc3[:, h, :], lg[:, h * 128:(h + 1) * 128], tri[:], True, True, [b_lg, b_tri], [b_pc])
                epos, b_epos = epos_r.next()
                eneg, b_eneg = eneg_r.next()
                S.op("act", lambda: nc.scalar.activation(epos, pc3, AF.Exp), reads=[b_pc], writes=[b_epos])
                S.op("act", lambda: nc.scalar.activation(eneg, pc3, AF.Exp, scale=-1.0), reads=[b_pc], writes=[b_eneg])
                t["epos"], t["b_epos"], t["eneg"], t["b_eneg"] = epos, b_epos, eneg, b_eneg

        def gla_kv(t, with_q):
            xnT, b_xnT = t["xnT"], t["b_xnT"]
            kfac, b_kfac = t["kfac"], t["b_kfac"]
            pk, b_pk = PS["g"].next()
            pk3 = v3(pk[:], 4)
            for h in range(4):
                for k in range(8):
                    mm(pk3[:, h, :], wgB[:, k, h * 128:(h + 1) * 128], xnT[:, k, :], k == 0, k == 7,
                       [b_xnT, b_wgB], [b_pk])
            kdT, b_kdT = kdT_r.next()
            S.op("dve", lambda: nc.vector.tensor_tensor(kdT, pk3, kfac, ALU.mult),
                 reads=[b_pk, b_kfac], writes=[b_kdT])
            t["kdT"], t["b_kdT"] = kdT, b_kdT
            if with_q:
                epos, b_epos, eneg, b_eneg = t["epos"], t["b_epos"], t["eneg"], t["b_eneg"]
                qk3, b_qk3 = qk3_r.next()
                qpos, b_qpos = qpos_r.next()
                t["qk3"], t["b_qk3"], t["qpos"], t["b_qpos"] = qk3, b_qk3, qpos, b_qpos
                S.op("dve", lambda: nc.vector.tensor_tensor(qk3[:, 1, :, :], pk3, eneg, ALU.mult),
                     reads=[b_pk, b_eneg], writes=[b_qk3])
                S.op("dve", lambda: nc.vector.tensor_tensor(qk3[:, 2, :, :], pk3, epos, ALU.mult),
                     reads=[b_pk, b_epos], writes=[b_qk3])
                pq, b_pq = PS["g"].next()
                pq3 = v3(pq[:], 4)
                for h in range(4):
                    for k in range(8):
                        mm(pq3[:, h, :], wgA[:, k, h * 128:(h + 1) * 128], xnT[:, k, :], k == 0, k == 7,
                           [b_xnT, b_wgA], [b_pq])
                S.op("dve", lambda: nc.vector.scalar_tensor_tensor(qpos, pq3, QSCALE, epos, ALU.mult, ALU.mult),
                     reads=[b_pq, b_epos], writes=[b_qpos])
                S.op("dve", lambda: nc.vector.scalar_tensor_tensor(qk3[:, 0, :, :], pq3, QSCALE, eneg, ALU.mult, ALU.mult),
                     reads=[b_pq, b_eneg], writes=[b_qk3])
        def gla_v(t):
            xnT, b_xnT = t["xnT"], t["b_xnT"]
            vt, b_vt = vtok_r.next()
            t["vt"], t["b_vt"] = vt, b_vt
            for nb in range(2):
                pv, b_pv = PS["g"].next()
                for k in range(8):
                    mm(pv[:], xnT[:, k, :], wgB[:, k, 512 + nb * 512:512 + (nb + 1) * 512], k == 0, k == 7,
                       [b_xnT, b_wgB], [b_pv])
                S.op("act", lambda pv=pv, nb=nb: nc.scalar.copy(vt[:, nb * 512:(nb + 1) * 512], pv[:]),
                     reads=[b_pv], writes=[b_vt])

        def gla_kdtok(t):
            kdT, b_kdT = t["kdT"], t["b_kdT"]
            ptk, b_ptk = PS["g"].next()
            ptk3 = v3(ptk[:], 4)
            for h in range(4):
                mm(ptk3[:, h, :], kdT[:, h, :], ident[:], True, True, [b_kdT, b_ident], [b_ptk])
            kdtok, b_kdtok = kdtok_r.next()
            S.op("act", lambda: nc.scalar.copy(kdtok, ptk3), reads=[b_ptk], writes=[b_kdtok])
            t["kdtok"], t["b_kdtok"] = kdtok, b_kdtok

        def gla_state(t):
            kdtok, b_kdtok, vt, b_vt = t["kdtok"], t["b_kdtok"], t["vt"], t["b_vt"]
            dfac, b_dfac = t["dfac"], t["b_dfac"]
            for hb in range(2):
                pS, b_pS = PS["g"].next()
                pS3 = v3(pS[:], 2)
                for h2 in range(2):
                    h = hb * 2 + h2
                    mm(pS3[:, h2, :], kdtok[:, h, :], vt[:, h * 256:(h + 1) * 256], True, True,
                       [b_kdtok, b_vt], [b_pS])
                for h2 in range(2):
                    h = hb * 2 + h2
                    S.op("dve", lambda pS3=pS3, h2=h2, h=h: nc.vector.scalar_tensor_tensor(
                        Sst[:, h, :], Sst[:, h, :], dfac[:, h, 0:1], pS3[:, h2, :], ALU.mult, ALU.add),
                        reads=[b_Sh[h], b_dfac, b_pS], writes=[b_Sh[h]])

        def after_layer0():
            fence([b_win0u, b_win0g, b_wgA] + [b_ for (_, b_) in qk3_r.items] + [b_ for (_, b_) in qp_ra])
            gwA = DmaGroup("wgA")
            for k in range(8):
                S.dma("pool", lambda k=k: nc.gpsimd.dma_start(out=wgA[:, k, 0:512], in_=gin_v[:, k, 0:512]),
                      writes=[b_wgA], group=gwA)
                S.dma("pool", lambda k=k: nc.gpsimd.dma_start(out=wgA[:, k, 512:1536], in_=gin_v[:, k, 2048:3072]),
                      writes=[b_wgA], group=gwA)
            fence([b_gw, b_t12] + [b_ for (_, b_) in sc_r.items])
            fence([b_wout0, b_wout1])
            gwO = DmaGroup("wout1")
            for k in range(8):
                S.dma("pool", lambda k=k: nc.gpsimd.dma_start(out=w_out1[:, k, :], in_=gow_v[:, k, :]),
                      writes=[b_wout1], group=gwO)
            cdma(mf[:], d_mf, b_mf, grp=glate_d["pool"])
            cdma(mb[:], d_mb, b_mb, grp=glate_d["pool"])
            cdma(fnwbc[:], d_fnw.partition_broadcast(128), b_fnwbc, "sp", grp=glate_d["sp"])

        def fold_hnw():
            for k in range(8):
                S.op("act", lambda k=k: nc.scalar.activation(w_out1[:, k, :], w_out1[:, k, :], AF.Copy,
                                                             scale=hnw2[:, (k % 2):(k % 2) + 1]),
                     reads=[b_wout1, b_hnw2], writes=[b_wout1])

        mf, b_mf = pcur, b_pcur
        mb, b_mb = pprev, b_pprev
        fnwbc, b_fnwbc = nwbc0, b_nwbc0

        t2 = {}

        def p2_norm_a(i):
            t2[i] = {}
            norm_a(t2[i], h1[:, i, :], b_h1[i], nwbc1, b_nwbc1)

        def p2_state(i):
            gla_state(t2[i])
            del t2[i]

        def before_prepass():
            fence(xt_all_bufs)
            S.op("pool", lambda: nc.gpsimd.memset(Sst[:], 0.0), writes=b_Sh)

        p2_stages = [(lambda i: gla_gk_a(t2[i]), 2), (lambda i: norm_t(t2[i]), 1), (p2_norm_a, 0),
                     (lambda i: gla_kv(t2[i], False), 3), (lambda i: gla_gk_b(t2[i]), 2),
                     (lambda i: gla_v(t2[i]), 3), (lambda i: norm_d(t2[i]), 0),
                     (lambda i: gla_kdtok(t2[i]), 4), (p2_state, 5),
                     (lambda i: gla_gk_c(t2[i], False), 2)]
        P2OFF = NT + 2
        run_multi([(p1_stages(), 0, NT + 1, 0), (p2_stages, 0, NT, P2OFF)],
                  {P2OFF: before_prepass, NT + 7: after_layer0})

        b_xin, b_xout = Buf("xin"), Buf("xout")
        def send_state():
            S.op("act", lambda: nc.scalar.activation(Sst[:], Sst[:], AF.Copy, scale=maskA[:, 0:1]),
                 reads=b_Sh + [b_maskA], writes=b_Sh)
            S.dma("pool", lambda: nc.gpsimd.dma_start(out=d_xin_t[:, :], in_=Sst[:].rearrange("p a b -> p (a b)")),
                  reads=b_Sh, writes=[b_xin], buf=b_xin)
            S.dma("pool", lambda: nc.gpsimd.collective_compute(
                "AllReduce", ALU.add, replica_groups=[[0, 1], [2, 3], [4, 5], [6, 7]],
                ins=[d_xin_t.ap().opt()], outs=[d_xout_t.ap().opt()]),
                reads=[b_xin], writes=[b_xout], buf=b_xout, inc=1)

        def recv_state():
            S.dma("pool", lambda: nc.gpsimd.dma_start(out=Sst[:].rearrange("p a b -> p (a b)"), in_=d_xout_t[:, :]),
                  reads=[b_xout], writes=b_Sh, buf=b_S)
            S.op("act", lambda: nc.scalar.activation(Sst[:], Sst[:], AF.Copy, scale=maskB[:, 0:1]),
                 reads=b_Sh + [b_maskB], writes=b_Sh)
            S.op("pool", lambda: nc.gpsimd.tensor_copy(Sbf[:], Sst[:]), reads=b_Sh, writes=[b_Sbf])

        PS["g"] = psg
        t3 = {}

        def p3_norm_a(i):
            t3[i] = {}
            norm_a(t3[i], h1[:, i, :], b_h1[i], nwbc1, b_nwbc1)
            if i == 0:
                send_state()

        def p3_gate(i):
            t = t3[i]
            xnT, b_xnT = t["xnT"], t["b_xnT"]
            sgt, b_sgt = sgt_r.next()
            t["sgt"], t["b_sgt"] = sgt, b_sgt
            for nb in range(2):
                pg, b_pg = PS["g"].next()
                for k in range(8):
                    mm(pg[:], xnT[:, k, :], wgA[:, k, 512 + nb * 512:512 + (nb + 1) * 512], k == 0, k == 7,
                       [b_xnT, b_wgA], [b_pg])
                S.op("act", lambda pg=pg, nb=nb: nc.scalar.activation(sgt[:, nb * 512:(nb + 1) * 512], pg[:], AF.Silu),
                     reads=[b_pg], writes=[b_sgt])

        def p3_scores(i):
            t = t3[i]
            if i == 0:
                recv_state()
            qk3, b_qk3, qpos, b_qpos = t["qk3"], t["b_qk3"], t["qpos"], t["b_qpos"]
            pf, b_pf = PS["g"].next()
            pb, b_pb = PS["g"].next()
            pf3, pb3 = v3(pf[:], 4), v3(pb[:], 4)
            for h in range(4):
                mm(pf3[:, h, :], qk3[:, 1, h, :], qpos[:, h, :], True, True, [b_qk3, b_qpos], [b_pf])
            for h in range(4):
                mm(pb3[:, h, :], qk3[:, 2, h, :], qk3[:, 0, h, :], True, True, [b_qk3], [b_pb])
            sc, b_sc = sc_r.next()
            t["sc"], t["b_sc"] = sc, b_sc
            S.op("dve", lambda: nc.vector.tensor_tensor(t12[:, 0, :, :], pf3, mf[:], ALU.mult),
                 reads=[b_pf, b_mf], writes=[b_t12])
            S.op("dve", lambda: nc.vector.tensor_tensor(t12[:, 1, :, :], pb3, mb[:], ALU.mult),
                 reads=[b_pb, b_mb], writes=[b_t12])
            S.op("pool", lambda: nc.gpsimd.tensor_tensor(sc, t12[:, 0, :, :], t12[:, 1, :, :], ALU.add),
                 reads=[b_t12], writes=[b_sc])
            gla_kdtok(t)

        def p3_o(i):
            t = t3[i]
            qpos, b_qpos, sc, b_sc, vt, b_vt = t["qpos"], t["b_qpos"], t["sc"], t["b_sc"], t["vt"], t["b_vt"]
            sgt, b_sgt = t["sgt"], t["b_sgt"]
            stt, b_st = stats.next()
            y, b_y = y_r.next()
            pos = []
            for hb in range(2):
                po, b_po = pso.next()
                po3 = v3(po[:], 2)
                for h2 in range(2):
                    h = hb * 2 + h2
                    mm(po3[:, h2, :], sc[:, h, :], vt[:, h * 256:(h + 1) * 256], True, False,
                       [b_sc, b_vt], [b_po])
                    mm(po3[:, h2, :], qpos[:, h, :], Sbf[:, h, :], False, True, [b_qpos, b_Sbf], [b_po])
                for h2 in range(2):
                    h = hb * 2 + h2
                    S.op("act", lambda po3=po3, h2=h2, h=h: nc.scalar.activation(
                        junk[:, 0:256], po3[:, h2, :], AF.Square, accum_out=stt[:, h:h + 1]),
                        reads=[b_po], writes=[b_st])
                pos.append((po3, b_po))
            gla_state(t)
            S.op("pool", lambda: nc.gpsimd.tensor_copy(Sbf[:], Sst[:]), reads=b_Sh, writes=[b_Sbf])
            S.op("act", lambda: nc.scalar.activation(stt[:, 4:8], stt[:, 0:4], AF.Ln, bias=EPS, scale=1.0 / 256),
                 reads=[b_st], writes=[b_st])
            S.op("act", lambda: nc.scalar.activation(stt[:, 0:4], stt[:, 4:8], AF.Exp, scale=-0.5),
                 reads=[b_st], writes=[b_st])
            for hb in range(2):
                po3, b_po = pos[hb]
                for h2 in range(2):
                    h = hb * 2 + h2
                    S.op("dve", lambda po3=po3, h2=h2, h=h: nc.vector.scalar_tensor_tensor(
                        y[:, h * 256:(h + 1) * 256], po3[:, h2, :], stt[:, h:h + 1], sgt[:, h * 256:(h + 1) * 256],
                        ALU.mult, ALU.mult),
                        reads=[b_po, b_st, b_sgt], writes=[b_y])
            t["y"], t["b_y"] = y, b_y

        def p3_yT(i):
            t = t3[i]
            if i == 0:
                fold_hnw()
            y, b_y = t["y"], t["b_y"]
            yT, b_yT = yT_r.next()
            for hb in range(2):
                pT, b_pT = PS["g"].next()
                pT3 = v3(pT[:], 4)
                for c4 in range(4):
                    c = hb * 4 + c4
                    mm(pT3[:, c4, :], y[:, c * 128:(c + 1) * 128], ident[:], True, True, [b_y, b_ident], [b_pT])
                S.op("dve", lambda pT3=pT3, hb=hb: nc.vector.tensor_copy(yT[:, hb * 4:(hb + 1) * 4, :], pT3),
                     reads=[b_pT], writes=[b_yT])
            t["yT"], t["b_yT"] = yT, b_yT

        def p3_out(i):
            t = t3[i]
            yT, b_yT = t["yT"], t["b_yT"]
            h2, b_h2 = h1[:, i, :], b_h1[i]
            for nb in range(2):
                po, b_po = PS["g"].next()
                for k in range(8):
                    mm(po[:], yT[:, k, :], w_out1[:, k, nb * 512:(nb + 1) * 512], k == 0, k == 7,
                       [b_yT, b_wout1], [b_po])
                S.op("dve", lambda po=po, nb=nb: nc.vector.tensor_tensor(
                    h2[:, nb * 512:(nb + 1) * 512], po[:], h2[:, nb * 512:(nb + 1) * 512], ALU.add),
                    reads=[b_po, b_h2], writes=[b_h2])
            stt, b_st = stats.next()
            S.op("act", lambda: nc.scalar.activation(junk[:], h2, AF.Square, accum_out=stt[:, 0:1]),
                 reads=[b_h2], writes=[b_st])
            S.op("act", lambda: nc.scalar.activation(stt[:, 1:2], stt[:, 0:1], AF.Ln, bias=EPS, scale=1.0 / D),
                 reads=[b_st], writes=[b_st])
            S.op("act", lambda: nc.scalar.activation(stt[:, 2:3], stt[:, 1:2], AF.Exp, scale=-0.5),
                 reads=[b_st], writes=[b_st])
            S.op("act", lambda: nc.scalar.activation(h2, h2, AF.Copy, scale=stt[:, 2:3]),
                 reads=[b_h2, b_st], writes=[b_h2])
            if i == NT - 1:
                S.op("dve", lambda: nc.vector.tensor_tensor(h2, h2, fnwbc[:], ALU.mult),
                     reads=[b_h2, b_fnwbc], writes=[b_h2])
            else:
                S.op("pool", lambda: nc.gpsimd.tensor_tensor(h2, h2, fnwbc[:], ALU.mult),
                     reads=[b_h2, b_fnwbc], writes=[b_h2])
            S.dma("sp", lambda: nc.sync.dma_start(out=d_out[i * 128:(i + 1) * 128, :], in_=h2),
                  reads=[b_h2], buf=b_h2)
            del t3[i]

        run_pipe([(lambda i: gla_kv(t3[i], True), 3), (lambda i: gla_gk_a(t3[i]), 2),
                  (lambda i: norm_t(t3[i]), 1), (p3_norm_a, 0),
                  (lambda i: gla_v(t3[i]), 3), (lambda i: gla_gk_b(t3[i]), 2), (p3_gate, 3),
                  (p3_scores, 4), (lambda i: norm_d(t3[i]), 0), (p3_o, 5),
                  (lambda i: gla_gk_c(t3[i], True), 2), (p3_out, 7), (p3_yT, 6)], 0, NT)

        S.lower()
    return nc


def _consts():
    c = {}
    c["ident"] = np.eye(128, dtype=np.float32)
    c["ones"] = np.ones((1, 128), np.float32)
    s = np.arange(128)[:, None]
    t = np.arange(128)[None, :]
    c["tri"] = np.where(s <= t, -1.0 / 16, 0.0).astype(np.float32)
    c["trirev"] = np.where(s > t, -1.0 / 16, 0.0).astype(np.float32)
    c["tot"] = np.full((128, 2), -1.0 / 16, np.float32)
    mf = (s <= t).astype(np.float32)
    mb = ((s > t) & ((s // 64) == (t // 64))).astype(np.float32)
    c["mf"] = np.ascontiguousarray(np.broadcast_to(mf[:, None, :], (128, 4, 128)))
    c["mb"] = np.ascontiguousarray(np.broadcast_to(mb[:, None, :], (128, 4, 128)))
    pcur = np.zeros((128, 4, 128), np.float32)
    pcurf = np.zeros((128, 4, 128), np.float32)
    pprev = np.zeros((128, 4, 128), np.float32)
    for g, w in enumerate((2, 4, 8, 16)):
        dist = t - s
        band = (dist >= 0) & (dist < w)
        pcur[:, g, :] = np.where(band, 1.0 / w, 0.0) - (dist == 0)
        cnt = np.minimum(t + 1, w).astype(np.float32)
        pcurf[:, g, :] = np.where(band, 1.0 / cnt, 0.0) - (dist == 0)
        dprev = t + 128 - s
        pprev[:, g, :] = np.where((dprev < w) & (s >= 64), 1.0 / w, 0.0)
    c["pcur"], c["pcurf"], c["pprev"] = pcur, pcurf, pprev
    return c


_NC = {}


def _get(mode):
    if mode not in _NC:
        _NC[mode] = build(mode)
    return _NC[mode]


def _maps_fused(x, norm_w, pool_in_w, pool_group_w, pool_group_b, pool_scale, pool_out_w,
                gla_in_w, gla_gk_w, gla_gk_b, gla_head_norm_w, gla_out_w, final_norm_w):
    f = lambda a: np.ascontiguousarray(np.asarray(a, dtype=np.float32))
    x = f(x)
    c = _consts()
    common = dict(ident=c["ident"], norm_w=f(norm_w), gla_in_w=f(gla_in_w[0]), gla_gk_w=f(gla_gk_w[0]),
                  gla_gk_b=f(gla_gk_b[0]).reshape(1, 512), ones=c["ones"], tri=c["tri"], trirev=c["trirev"],
                  tot=c["tot"], pool_in_w=f(pool_in_w[0]), pool_group_w=f(pool_group_w[0]),
                  pool_group_b=f(pool_group_b[0]).reshape(D), pool_scale=f(pool_scale[0]).reshape(D),
                  pool_out_w=f(pool_out_w[0]), pcur=c["pcur"], pprev=c["pprev"],
                  hnw2=np.ascontiguousarray(f(gla_head_norm_w[0]).reshape(2, 128).T),
                  gla_out_w=f(gla_out_w[0]),
                  final_norm_w=f(final_norm_w).reshape(D), mf=c["mf"], mb=c["mb"])
    maps = []
    for core in range(8):
        b, half = core // 2, core % 2
        own = x[b, half * TOK:(half + 1) * TOK]
        halo = np.zeros((128, D), np.float32) if half == 0 else x[b, TOK - 128:TOK]
        m = dict(common)
        m.update(x=np.ascontiguousarray(np.concatenate([halo, own], 0)),
                 pcurf=c["pcurf"] if half == 0 else c["pcur"],
                 mask_a=np.full((128, 1), 1.0 if half == 0 else 0.0, np.float32),
                 mask_b=np.full((128, 1), 1.0 if half == 1 else 0.0, np.float32))
        maps.append(m)
    return maps


def kernel(x, norm_w, pool_in_w, pool_group_w, pool_group_b, pool_scale, pool_out_w,
           gla_in_w, gla_gk_w, gla_gk_b, gla_head_norm_w, gla_out_w, final_norm_w):
    maps = _maps_fused(x, norm_w, pool_in_w, pool_group_w, pool_group_b, pool_scale, pool_out_w,
                       gla_in_w, gla_gk_w, gla_gk_b, gla_head_norm_w, gla_out_w, final_norm_w)
    res = run_bass_kernel_spmd(_get("F"), maps, core_ids=list(range(8))).results
    out = np.empty((4, 2 * TOK, D), np.float32)
    for core in range(8):
        b, half = core // 2, core % 2
        out[b, half * TOK:(half + 1) * TOK] = res[core]["out"]
    return out
```

```python
from contextlib import ExitStack

import numpy as np

import concourse.bass as bass
import concourse.mybir as mybir
from concourse.bass_utils import run_bass_kernel_spmd

F32 = mybir.dt.float32
BF16 = mybir.dt.bfloat16
AF = mybir.ActivationFunctionType
ALU = mybir.AluOpType

D = 1024
NT = 16
TOK = NT * 128
EPS = 1e-6
QSCALE = 128 ** -0.5


class Buf:
    __slots__ = ("name", "last_w", "readers", "sem")

    def __init__(self, name):
        self.name = name
        self.last_w = None
        self.readers = []
        self.sem = None


class DmaGroup:
    def __init__(self, name):
        self.name = name
        self.sem = None
        self.total = 0


class Op:
    __slots__ = ("eng", "fn", "deps", "signal", "val", "sem", "is_dma", "buf", "group", "inc")

    def __init__(self, eng, fn):
        self.eng = eng
        self.fn = fn
        self.deps = []
        self.signal = False
        self.val = None
        self.sem = None
        self.is_dma = False
        self.buf = None
        self.group = None


class Sched:
    ENGS = ("pe", "act", "dve", "pool", "sp")

    def __init__(self, nc, stack):
        self.nc = nc
        self.stack = stack
        self.ops = {e: [] for e in self.ENGS}
        self.engobj = {"pe": nc.tensor, "act": nc.scalar, "dve": nc.vector,
                       "pool": nc.gpsimd, "sp": nc.sync}
        self.all_ops = []

    def _track(self, op, reads, writes):
        deps = op.deps
        for b in reads:
            if b.last_w is not None:
                deps.append(b.last_w)
        for b in writes:
            if b.last_w is not None:
                deps.append(b.last_w)
            deps.extend(b.readers)
        for b in reads:
            b.readers.append(op)
        for b in writes:
            b.last_w = op
            b.readers = []
        self.ops[op.eng].append(op)
        self.all_ops.append(op)

    def op(self, eng, fn, reads=(), writes=()):
        o = Op(eng, fn)
        self._track(o, reads, writes)
        return o

    def dma(self, eng, fn, reads=(), writes=(), buf=None, group=None, inc=16):
        o = Op(eng, fn)
        o.inc = inc
        o.is_dma = True
        o.buf = buf
        o.group = group
        self._track(o, reads, writes)
        return o

    def lower(self):
        nc = self.nc
        for o in self.all_ops:
            for d in o.deps:
                if d.is_dma:
                    continue
                if d.eng == "pe" and o.eng == "pe" and not o.is_dma:
                    continue
                d.signal = True
        esem = {e: self.stack.enter_context(nc.semaphore("sem_" + e)) for e in self.ENGS}
        cnt = {e: 0 for e in esem}
        dma_cnt = {}
        for o in self.all_ops:
            if o.is_dma:
                if o.group is not None:
                    g = o.group
                    if g.sem is None:
                        g.sem = self.stack.enter_context(nc.semaphore("g_" + g.name))
                    g.total += o.inc
                    o.sem = g.sem
                else:
                    b = o.buf
                    if b.sem is None:
                        b.sem = self.stack.enter_context(nc.semaphore("b_" + b.name))
                        dma_cnt[b] = 0
                    dma_cnt[b] += o.inc
                    o.sem = b.sem
                    o.val = dma_cnt[b]
            else:
                o.sem = esem[o.eng]
                if o.signal:
                    cnt[o.eng] += 1
                    o.val = cnt[o.eng]
        for o in self.all_ops:
            if o.is_dma and o.group is not None:
                o.val = o.group.total
        final = {}
        for e, lst in self.ops.items():
            eng = self.engobj[e]
            waited = {}
            for o in lst:
                need = {}
                for d in o.deps:
                    if (not d.is_dma) and d.eng == "pe" and e == "pe" and not o.is_dma:
                        continue
                    if d.is_dma and o.is_dma and d.group is not None and d.group is o.group:
                        continue
                    if need.get(d.sem, (0, None))[0] < d.val:
                        need[d.sem] = (d.val, d.sem)
                for key, (v, s) in need.items():
                    if waited.get(key, 0) >= v:
                        continue
                    eng.wait_ge(s, v)
                    waited[key] = v
                ins = o.fn()
                if o.is_dma:
                    ins.then_inc(o.sem, o.inc)
                    final[o.sem] = (o.sem, o.val)
                elif o.signal:
                    ins.then_inc(o.sem, 1)
        eng = self.engobj["sp"]
        for s, v in final.values():
            eng.wait_ge(s, v)


class Rot:
    def __init__(self, items):
        self.items = items
        self.i = 0

    def next(self):
        it = self.items[self.i % len(self.items)]
        self.i += 1
        return it


def build(mode="F"):
    nc = bass.Bass("TRN2", target_bir_lowering=False)
    dt_in = lambda n, s: nc.dram_tensor(n, s, F32, kind="ExternalInput").ap()
    dt_out = lambda n, s: nc.dram_tensor(n, s, F32, kind="ExternalOutput").ap()

    d_ident = dt_in("ident", [128, 128])
    d_normw = dt_in("norm_w", [2, D])
    d_x = dt_in("x", [128 + TOK, D])
    d_inw = dt_in("pool_in_w", [D, 2 * D])
    d_gw = dt_in("pool_group_w", [4, 256, 256])
    d_gb = dt_in("pool_group_b", [D])
    d_sc = dt_in("pool_scale", [D])
    d_ow = dt_in("pool_out_w", [D, D])
    d_pcur = dt_in("pcur", [128, 4, 128])
    d_pcurf = dt_in("pcurf", [128, 4, 128])
    d_maskA = dt_in("mask_a", [128, 1])
    d_maskB = dt_in("mask_b", [128, 1])
    d_pprev = dt_in("pprev", [128, 4, 128])
    d_gin = dt_in("gla_in_w", [D, 3088])
    d_gkw = dt_in("gla_gk_w", [16, 512])
    d_gkb = dt_in("gla_gk_b", [1, 512])
    d_ones = dt_in("ones", [1, 128])
    d_tri = dt_in("tri", [128, 128])
    d_trirev = dt_in("trirev", [128, 128])
    d_tot = dt_in("tot", [128, 2])
    d_hnw = dt_in("hnw2", [128, 2])
    d_gow = dt_in("gla_out_w", [D, D])
    d_fnw = dt_in("final_norm_w", [D])
    d_mf = dt_in("mf", [128, 4, 128])
    d_mb = dt_in("mb", [128, 4, 128])
    d_out = dt_out("out", [TOK, D])
    d_xin_t = nc.dram_tensor("xchg_in", [128, 1024], F32)
    d_xout_t = nc.dram_tensor("xchg_out", [128, 1024], F32)

    with ExitStack() as st:
        S = Sched(nc, st)

        def sb(name, shape, dt):
            return st.enter_context(nc.sbuf_tensor(name, shape, dt)), Buf(name)

        def sbn(name, shape, dt, n):
            return Rot([sb("%s%d" % (name, i), shape, dt) for i in range(n)])

        def apn(rot):
            return Rot([(t_[:], b_) for (t_, b_) in rot.items])

        _ps8 = [(st.enter_context(nc.psum_tensor("psf%d" % i, [128, 512], F32)), Buf("psf%d" % i))
                for i in range(8)]
        psf = Rot(_ps8)
        psg = Rot(_ps8[0:6])
        pso = Rot(_ps8[6:8])

        def v3(ap, a):
            return ap.rearrange("p (a b) -> p a b", a=a)

        h1, _ = sb("h1sb", [128, NT, D], F32)
        b_h1 = [Buf("h1_%d" % i) for i in range(NT)]
        RA, _ = sb("RA", [128, 26624], BF16)
        w_in0 = RA[:, 0:16384].rearrange("p (k n) -> p k n", k=8)
        gw = RA[:, 16384:18432].rearrange("p (g i n) -> p g i n", g=4, i=2)
        w_out0 = RA[:, 18432:26624].rearrange("p (k n) -> p k n", k=8)
        b_win0, b_gw, b_wout0 = Buf("w_in0"), Buf("gw"), Buf("w_out0")
        wgA = RA[:, 0:12288].rearrange("p (k n) -> p k n", k=8)
        b_wgA = Buf("wgA")
        qk3_r = Rot([(RA[:, 12288 + j * 1536:12288 + (j + 1) * 1536].rearrange("p (v h t) -> p v h t", v=3, h=4),
                      Buf("qk3_%d" % j)) for j in range(2)])
        qp_ra = [(RA[:, 15360 + j * 512:15360 + (j + 1) * 512].rearrange("p (h t) -> p h t", h=4), Buf("qpos%d" % j))
                 for j in range(2)]
        t12, b_t12 = RA[:, 16384:17408].rearrange("p (v h t) -> p v h t", v=2, h=4), Buf("t12")
        sc_r = Rot([(RA[:, 17408 + j * 512:17408 + (j + 1) * 512].rearrange("p (h t) -> p h t", h=4),
                     Buf("sc%d" % j)) for j in range(2)])
        w_out1 = RA[:, 18432:26624].rearrange("p (k n) -> p k n", k=8)
        b_wout1 = Buf("w_out1")
        wgB, b_wgB = sb("wgB", [128, 8, 1552], BF16)
        nwbc0, b_nwbc0 = sb("nwbc0", [128, D], F32)
        nwbc1, b_nwbc1 = sb("nwbc1", [128, D], BF16)
        ident, b_ident = sb("identb", [128, 128], BF16)
        gkwb, b_gkwb = sb("gkwb", [128, 512], BF16)
        tri, b_tri = sb("trib", [128, 128], BF16)
        trirev, b_trirev = sb("trirevb", [128, 128], BF16)
        tot, b_tot = sb("totb", [128, 2], BF16)
        pcur, b_pcur = sb("pcurb", [128, 4, 128], BF16)
        pcurf, b_pcurf = sb("pcurfb", [128, 4, 128], BF16)
        pprev, b_pprev = sb("pprevb", [128, 4, 128], BF16)
        maskA, b_maskA = sb("maskA", [128, 1], F32)
        maskB, b_maskB = sb("maskB", [128, 1], F32)
        gb, b_gb = sb("gb", [128, 8], F32)
        scl, b_scl = sb("scl", [128, 8], F32)
        hnw2, b_hnw2 = sb("hnw2sb", [128, 2], F32)
        Sst, b_S = sb("Sst", [128, 4, 256], F32)
        b_Sh = [Buf("S_h%d" % h_) for h_ in range(4)]
        Sbf, b_Sbf = sb("Sbf", [128, 4, 256], BF16)
        junk, _ = sb("junk", [128, D], BF16)
        fdummy, _ = sb("fdummy", [128, 16], F32)
        fcount = [0]
        stats = sbn("stats", [128, 8], F32, 8)
        xt_r = sbn("xt", [128, D], F32, 3)
        xn_r = sbn("xn", [128, D], BF16, 1)
        xnT_r = sbn("xnT", [128, 8, 128], BF16, 3)
        u_r = sbn("utok", [128, D], BF16, 3)
        sg_r = sbn("sg", [128, 8, 128], BF16, 3)
        pl_r = sbn("pooledT", [128, 8, 128], BF16, 2)
        yT_r = sbn("yT", [128, 8, 128], BF16, 2)
        kdT_r = apn(sbn("kdT", [128, 4, 128], BF16, 2))
        kdtok_r = apn(sbn("kdtok", [128, 4, 128], BF16, 2))
        lg_r = apn(sbn("lg", [128, 512], BF16, 1))
        qp3, b_qp3 = sb("qpos2", [128, 4, 128], BF16)
        qpos_r = Rot(qp_ra + [(qp3[:], b_qp3)])
        gkT_r = sbn("gkT", [128, 128], BF16, 2)
        dfac_r = sbn("dfac", [128, 4, 2], F32, 4)
        vtok_r = apn(u_r)
        sgt_r = Rot([(t_[:].rearrange("p a b -> p (a b)"), b_) for (t_, b_) in sg_r.items])
        y_r = Rot([(t_[:].rearrange("p a b -> p (a b)"), b_) for (t_, b_) in pl_r.items])
        xi = xt_r.items
        xsub = {}
        for j in range(3):
            for hlf in range(2):
                xsub[(j, hlf)] = (xi[j][0][:, hlf * 512:(hlf + 1) * 512].rearrange("p (a b) -> p a b", a=4),
                                  Buf("xt%d_%d" % (j, hlf)))
        epos_r = Rot([xsub[(0, 0)], xsub[(2, 0)]])
        eneg_r = Rot([xsub[(0, 1)], xsub[(2, 1)]])
        kfac_r = Rot([xsub[(1, 0)], xsub[(1, 1)]])
        xt_all_bufs = [b_ for (_, b_) in xi] + [b_ for (_, b_) in xsub.values()]

        def fence(bufs):
            k = fcount[0]
            fcount[0] += 1
            S.op("pool", lambda: nc.gpsimd.memset(fdummy[:, k:k + 1], 0.0), writes=bufs)

        gconst_d = {"pool": DmaGroup("const_pool"), "sp": DmaGroup("const_sp")}
        glate_d = {"pool": DmaGroup("late_pool"), "sp": DmaGroup("late_sp")}

        def cdma(out_ap, in_ap, b, eng="pool", slow=False, grp=None):
            e = nc.gpsimd if eng == "pool" else nc.sync
            g = grp if grp is not None else gconst_d[eng]
            if slow:
                S.dma(eng, lambda: e.dma_start(out=out_ap, in_=in_ap, allow_slow_non_contiguous=True),
                      writes=[b], group=g)
            else:
                S.dma(eng, lambda: e.dma_start(out=out_ap, in_=in_ap), writes=[b], group=g)

        cdma(ident[:], d_ident, b_ident)
        cdma(nwbc0[:], d_normw[0].partition_broadcast(128), b_nwbc0, "sp")

        def load_nwbc1():
            S.dma("sp", lambda: nc.sync.dma_start(out=h1[:, 15, :], in_=d_normw[1].partition_broadcast(128)),
                  writes=[b_h1[15]], buf=b_h1[15])
            S.op("pool", lambda: nc.gpsimd.tensor_copy(nwbc1[:], h1[:, 15, :]), reads=[b_h1[15]], writes=[b_nwbc1])
        cdma(maskA[:], d_maskA, b_maskA, "sp")
        cdma(maskB[:], d_maskB, b_maskB, "sp")
        cdma(gb[:], d_gb.rearrange("(k p) -> p k", p=128), b_gb, "sp", True)
        cdma(scl[:], d_sc.rearrange("(k p) -> p k", p=128), b_scl, "sp", True)
        gw0 = DmaGroup("w0")
        inw_v = d_inw.rearrange("(k p) n -> p k n", p=128)
        for k in range(8):
            S.dma("pool", lambda k=k: nc.gpsimd.dma_start(out=w_in0[:, k, :], in_=inw_v[:, k, :]),
                  writes=[b_win0], group=gw0)
        S.dma("pool", lambda: nc.gpsimd.dma_start(out=gw, in_=d_gw.rearrange("g (i p) n -> p g i n", p=128)),
              writes=[b_gw], group=gw0)
        cdma(pcur[:], d_pcur, b_pcur)
        cdma(pcurf[:], d_pcurf, b_pcurf)
        cdma(pprev[:], d_pprev, b_pprev)
        ow_v = d_ow.rearrange("(k p) n -> p k n", p=128)

        gstg = DmaGroup("stg_wout0")

        S.dma("sp", lambda: nc.sync.dma_start(out=h1[:, 14, :], in_=d_sc.partition_broadcast(128)),
              writes=[b_h1[14]], buf=b_h1[14])
        sbc3 = h1[:, 14, :].rearrange("p (g n) -> p g n", g=4)

        def fold_pool_scale():
            S.op("dve", lambda: nc.vector.tensor_tensor(gb[:], gb[:], scl[:], ALU.mult),
                 reads=[b_gb, b_scl], writes=[b_gb])
            for ic in range(2):
                S.op("dve", lambda ic=ic: nc.vector.tensor_tensor(gw[:, :, ic, :], gw[:, :, ic, :], sbc3, ALU.mult),
                     reads=[b_gw, b_h1[14]], writes=[b_gw])

        def stage_wout0_dma(after_buf):
            for k in range(8):
                S.dma("pool", lambda k=k: nc.gpsimd.dma_start(out=w_out0[:, k, :], in_=ow_v[:, k, :]),
                      reads=[after_buf], writes=[b_wout0], group=gstg)

        def stage_wout0_scale():
            for k in range(8):
                if k % 2 == 0:
                    S.op("act", lambda k=k: nc.scalar.activation(w_out0[:, k, :], h1[:, 8 + k, :], AF.Copy,
                                                                 scale=scl[:, k:k + 1]),
                         reads=[b_h1[8 + k], b_scl], writes=[b_wout0])
                else:
                    S.op("dve", lambda k=k: nc.vector.tensor_scalar(w_out0[:, k, :], h1[:, 8 + k, :],
                                                                    scl[:, k:k + 1], None, ALU.mult),
                         reads=[b_h1[8 + k], b_scl], writes=[b_wout0])
        S.op("pool", lambda: nc.gpsimd.memset(gkwb[:], 0.0), writes=[b_gkwb])
        for (g_, bg_) in gkT_r.items:
            S.op("pool", lambda g_=g_: nc.gpsimd.memset(g_[:], 0.0), writes=[bg_])
        cdma(gkwb[0:16, :], d_gkw, b_gkwb)
        cdma(gkwb[16:17, :], d_gkb, b_gkwb)
        for (g_, bg_) in gkT_r.items:
            cdma(g_[16:17, :], d_ones, bg_)
        cdma(tri[:], d_tri, b_tri)
        cdma(trirev[:], d_trirev, b_trirev)
        cdma(tot[:], d_tot, b_tot)
        cdma(hnw2[:], d_hnw, b_hnw2, "sp")
        gwB = DmaGroup("wgB")
        gin_v = d_gin.rearrange("(k p) n -> p k n", p=128)
        gow_v = d_gow.rearrange("(k p) n -> p k n", p=128)

        def load_wgB():
            for k in range(8):
                S.dma("pool", lambda k=k: nc.gpsimd.dma_start(out=wgB[:, k, 0:1536], in_=gin_v[:, k, 512:2048]),
                      writes=[b_wgB], group=gwB)
            S.dma("pool", lambda: nc.gpsimd.dma_start(out=wgB[:, :, 1536:1552], in_=gin_v[:, :, 3072:3088]),
                  writes=[b_wgB], group=gwB)

        PS = {"g": psf}

        def mm(out_ap, lhsT, rhs, start, stop, reads, writes):
            S.op("pe", lambda: nc.tensor.matmul(out_ap, lhsT=lhsT, rhs=rhs, start=start, stop=stop),
                 reads=reads, writes=writes)

        def norm_a(t, src_ap, b_src, nwbc, b_nw):
            stt, b_st = stats.next()
            S.op("act", lambda: nc.scalar.activation(junk[:], src_ap, AF.Square, accum_out=stt[:, 0:1]),
                 reads=[b_src], writes=[b_st])
            S.op("act", lambda: nc.scalar.activation(stt[:, 1:2], stt[:, 0:1], AF.Ln, bias=EPS, scale=1.0 / D),
                 reads=[b_st], writes=[b_st])
            S.op("act", lambda: nc.scalar.activation(stt[:, 2:3], stt[:, 1:2], AF.Exp, scale=-0.5),
                 reads=[b_st], writes=[b_st])
            t["nrm"] = (stt, b_st, src_ap, b_src, nwbc, b_nw)

        def norm_d(t):
            stt, b_st, src_ap, b_src, nwbc, b_nw = t["nrm"]
            xn, b_xn = xn_r.next()
            S.op("dve", lambda: nc.vector.scalar_tensor_tensor(xn[:], src_ap, stt[:, 2:3], nwbc[:], ALU.mult, ALU.mult),
                 reads=[b_src, b_st, b_nw], writes=[b_xn])
            t["xn"], t["b_xn"] = xn, b_xn

        def norm_t(t):
            xn, b_xn = t["xn"], t["b_xn"]
            xnT, b_xnT = xnT_r.next()
            for hb in range(2):
                pT, b_pT = PS["g"].next()
                pT3 = v3(pT[:], 4)
                for c4 in range(4):
                    c = hb * 4 + c4
                    mm(pT3[:, c4, :], xn[:, c * 128:(c + 1) * 128], ident[:], True, True, [b_xn, b_ident], [b_pT])
                S.op("dve", lambda pT3=pT3, hb=hb: nc.vector.tensor_copy(xnT[:, hb * 4:(hb + 1) * 4, :], pT3),
                     reads=[b_pT], writes=[b_xnT])
            t["xnT"], t["b_xnT"] = xnT, b_xnT

        def run_pipe(stages, lo, hi):
            maxlag = max(l for _, l in stages)
            for step in range(lo, hi + maxlag):
                for fn, lag in stages:
                    i = step - lag
                    if lo <= i < hi:
                        fn(i)

        tiles = {}
        P1CFG = {"base": 0}

        def p1_load(i):
            if i == 9:
                load_wgB()
                load_nwbc1()
            xt, b_xt = xt_r.next()
            S.dma("sp", lambda: nc.sync.dma_start(out=xt[:], in_=d_x[i * 128:(i + 1) * 128, :]),
                  writes=[b_xt], buf=b_xt)
            tiles[i] = dict(xt=xt, b_xt=b_xt)

        def p1_norm_a(i):
            t = tiles[i]
            norm_a(t, t["xt"][:], t["b_xt"], nwbc0, b_nwbc0)
            ti = i - 1 - P1CFG["base"]
            if ti >= 0:
                S.op("pool", lambda: nc.gpsimd.tensor_copy(h1[:, ti, :], t["xt"][:]),
                     reads=[t["b_xt"]], writes=[b_h1[ti]])

        def p1_norm_t(i):
            norm_t(tiles[i])

        def p1_proj(i):
            t = tiles[i]
            xnT, b_xnT = t["xnT"], t["b_xnT"]
            u, b_u = u_r.next()
            t["u"], t["b_u"] = u, b_u
            for nb in range(2):
                pu, b_pu = psf.next()
                for k in range(8):
                    mm(pu[:], xnT[:, k, :], w_in0[:, k, nb * 512:(nb + 1) * 512], k == 0, k == 7,
                       [b_xnT, b_win0], [b_pu])
                S.op("act", lambda pu=pu, nb=nb: nc.scalar.copy(u[:, nb * 512:(nb + 1) * 512], pu[:]),
                     reads=[b_pu], writes=[b_u])
            if i == P1CFG["base"]:
                fold_pool_scale()
                stage_wout0_dma(b_u)
                return
            sg, b_sg = sg_r.next()
            t["sg"], t["b_sg"] = sg, b_sg
            for hb in range(2):
                pg, b_pg = psf.next()
                pg3 = v3(pg[:], 4)
                for c4 in range(4):
                    cc = hb * 4 + c4
                    for k in range(8):
                        mm(pg3[:, c4, :], w_in0[:, k, D + cc * 128:D + (cc + 1) * 128], xnT[:, k, :],
                           k == 0, k == 7, [b_xnT, b_win0], [b_pg])
                S.op("act", lambda pg3=pg3, hb=hb: nc.scalar.activation(sg[:, hb * 4:(hb + 1) * 4, :], pg3, AF.Silu),
                     reads=[b_pg], writes=[b_sg])

        def p1_pool(i):
            t = tiles[i]
            u, b_u = t["u"], t["b_u"]
            up, b_up = tiles[i - 1]["u"], tiles[i - 1]["b_u"]
            pl, b_pl = pl_r.next()
            t["pl"], t["b_pl"] = pl, b_pl
            pc_t, b_pc = (pcurf, b_pcurf) if i == P1CFG["base"] + 1 else (pcur, b_pcur)
            for hb in range(2):
                pp, b_pp = psf.next()
                pp3 = v3(pp[:], 4)
                for c4 in range(4):
                    cc = hb * 4 + c4
                    g = cc // 2
                    mm(pp3[:, c4, :], u[:, cc * 128:(cc + 1) * 128], pc_t[:, g, :], True, False,
                       [b_u, b_pc], [b_pp])
                    mm(pp3[:, c4, 0:16], up[:, cc * 128:(cc + 1) * 128], pprev[:, g, 0:16], False, True,
                       [b_up, b_pprev], [b_pp])
                S.op("dve", lambda pp3=pp3, hb=hb: nc.vector.tensor_copy(pl[:, hb * 4:(hb + 1) * 4, :], pp3),
                     reads=[b_pp], writes=[b_pl])

        def p1_group(i):
            t = tiles[i]
            pl, b_pl, sg, b_sg = t["pl"], t["b_pl"], t["sg"], t["b_sg"]
            yT, b_yT = yT_r.next()
            t["yT"], t["b_yT"] = yT, b_yT
            for hb in range(2):
                pm, b_pm = psf.next()
                pm3 = v3(pm[:], 4)
                for c4 in range(4):
                    oc = hb * 4 + c4
                    g, j = oc // 2, oc % 2
                    for ic in range(2):
                        mm(pm3[:, c4, :], gw[:, g, ic, j * 128:(j + 1) * 128], pl[:, 2 * g + ic, :],
                           ic == 0, ic == 1, [b_gw, b_pl], [b_pm])
                for c4 in range(4):
                    oc = hb * 4 + c4
                    S.op("dve", lambda pm3=pm3, c4=c4, oc=oc: nc.vector.scalar_tensor_tensor(
                        yT[:, oc, :], pm3[:, c4, :], gb[:, oc:oc + 1], sg[:, oc, :], ALU.add, ALU.mult),
                        reads=[b_pm, b_gb, b_sg], writes=[b_yT])

        def p1_out(i):
            t = tiles[i]
            yT, b_yT = t["yT"], t["b_yT"]
            ti = i - 1 - P1CFG["base"]
            for nb in range(2):
                po, b_po = psf.next()
                for k in range(8):
                    mm(po[:], yT[:, k, :], w_out0[:, k, nb * 512:(nb + 1) * 512], k == 0, k == 7,
                       [b_yT, b_wout0], [b_po])
                S.op("dve", lambda po=po, nb=nb: nc.vector.tensor_tensor(
                    h1[:, ti, nb * 512:(nb + 1) * 512], po[:], h1[:, ti, nb * 512:(nb + 1) * 512], ALU.add),
                    reads=[b_po, b_h1[ti]], writes=[b_h1[ti]])
            del tiles[i - 1]

        def p1_stages():
            P1CFG["base"] = 0
            own = lambda f: (lambda i: f(i) if i > 0 else None)
            return [(p1_norm_t, 2), (p1_norm_a, 1), (p1_proj, 3), (own(p1_pool), 4),
                    (lambda i: norm_d(tiles[i]), 1),
                    (own(p1_group), 5), (own(p1_out), 6), (p1_load, 0)]

        def run_multi(pipes, hooks):
            last = max(off + hi - 1 + max(l for _, l in st_) for (st_, lo, hi, off) in pipes)
            last = max(last, max(hooks) if hooks else 0)
            for step in range(0, last + 1):
                if step in hooks:
                    hooks[step]()
                for (st_, lo, hi, off) in pipes:
                    for fn, lag in st_:
                        i = step - off - lag
                        if lo <= i < hi:
                            fn(i)

        def gla_gk_a(t):
            xnT, b_xnT = t["xnT"], t["b_xnT"]
            pgk, b_pgk = PS["g"].next()
            for k in range(8):
                mm(pgk[0:16, 0:128], wgB[:, k, 1536:1552], xnT[:, k, :], k == 0, k == 7,
                   [b_xnT, b_wgB], [b_pgk])
            gkT, b_gkT = gkT_r.next()
            S.op("act", lambda: nc.scalar.copy(gkT[0:16, :], pgk[0:16, 0:128]), reads=[b_pgk], writes=[b_gkT])
            t["gkT"], t["b_gkT"] = gkT, b_gkT

        def gla_gk_b(t):
            gkT, b_gkT = t["gkT"], t["b_gkT"]
            pz, b_pz = PS["g"].next()
            mm(pz[:], gkT[:], gkwb[:], True, True, [b_gkT, b_gkwb], [b_pz])
            lg, b_lg = lg_r.next()
            S.op("act", lambda: nc.scalar.activation(pz[:], pz[:], AF.Exp, scale=-1.0), reads=[b_pz], writes=[b_pz])
            S.op("act", lambda: nc.scalar.activation(lg, pz[:], AF.Ln, bias=1.0), reads=[b_pz], writes=[b_lg])
            t["lg"], t["b_lg"] = lg, b_lg

        def gla_gk_c(t, with_cum):
            lg, b_lg = t["lg"], t["b_lg"]
            prc, b_prc = PS["g"].next()
            prc3 = v3(prc[:], 4)
            for h in range(4):
                mm(prc3[:, h, :], lg[:, h * 128:(h + 1) * 128], trirev[:], True, True, [b_lg, b_trirev], [b_prc])
            kfac, b_kfac = kfac_r.next()
            S.op("act", lambda: nc.scalar.activation(kfac, prc3, AF.Exp), reads=[b_prc], writes=[b_kfac])
            pd, b_pd = PS["g"].next()
            pd3 = pd[:, 0:8].rearrange("p (a b) -> p a b", a=4)
            for h in range(4):
                mm(pd3[:, h, :], lg[:, h * 128:(h + 1) * 128], tot[:], True, True, [b_lg, b_tot], [b_pd])
            dfac, b_dfac = dfac_r.next()
            S.op("act", lambda: nc.scalar.activation(dfac[:], pd3, AF.Exp), reads=[b_pd], writes=[b_dfac])
            t["kfac"], t["b_kfac"], t["dfac"], t["b_dfac"] = kfac, b_kfac, dfac, b_dfac
            if with_cum:
                pc, b_pc = PS["g"].next()
                pc3 = v3(pc[:], 4)
                for h in range(4):
                    mm(pc3[:, h, :], lg[:, h * 128:(h + 1) * 128], tri[:], True, True, [b_lg, b_tri], [b_pc])
                epos, b_epos = epos_r.next()
                eneg, b_eneg = eneg_r.next()
                S.op("act", lambda: nc.scalar.activation(epos, pc3, AF.Exp), reads=[b_pc], writes=[b_epos])
                S.op("act", lambda: nc.scalar.activation(eneg, pc3, AF.Exp, scale=-1.0), reads=[b_pc], writes=[b_eneg])
                t["epos"], t["b_epos"], t["eneg"], t["b_eneg"] = epos, b_epos, eneg, b_eneg

        def gla_kv(t, with_q):
            xnT, b_xnT = t["xnT"], t["b_xnT"]
            kfac, b_kfac = t["kfac"], t["b_kfac"]
            pk, b_pk = PS["g"].next()
            pk3 = v3(pk[:], 4)
            for h in range(4):
                for k in range(8):
                    mm(pk3[:, h, :], wgB[:, k, h * 128:(h + 1) * 128], xnT[:, k, :], k == 0, k == 7,
                       [b_xnT, b_wgB], [b_pk])
            kdT, b_kdT = kdT_r.next()
            S.op("dve", lambda: nc.vector.tensor_tensor(kdT, pk3, kfac, ALU.mult),
                 reads=[b_pk, b_kfac], writes=[b_kdT])
            t["kdT"], t["b_kdT"] = kdT, b_kdT
            if with_q:
                epos, b_epos, eneg, b_eneg = t["epos"], t["b_epos"], t["eneg"], t["b_eneg"]
                qk3, b_qk3 = qk3_r.next()
                qpos, b_qpos = qpos_r.next()
                t["qk3"], t["b_qk3"], t["qpos"], t["b_qpos"] = qk3, b_qk3, qpos, b_qpos
                S.op("dve", lambda: nc.vector.tensor_tensor(qk3[:, 1, :, :], pk3, eneg, ALU.mult),
                     reads=[b_pk, b_eneg], writes=[b_qk3])
                S.op("dve", lambda: nc.vector.tensor_tensor(qk3[:, 2, :, :], pk3, epos, ALU.mult),
                     reads=[b_pk, b_epos], writes=[b_qk3])
                pq, b_pq = PS["g"].next()
                pq3 = v3(pq[:], 4)
                for h in range(4):
                    for k in range(8):
                        mm(pq3[:, h, :], wgA[:, k, h * 128:(h + 1) * 128], xnT[:, k, :], k == 0, k == 7,
                           [b_xnT, b_wgA], [b_pq])
                S.op("dve", lambda: nc.vector.scalar_tensor_tensor(qpos, pq3, QSCALE, epos, ALU.mult, ALU.mult),
                     reads=[b_pq, b_epos], writes=[b_qpos])
                S.op("dve", lambda: nc.vector.scalar_tensor_tensor(qk3[:, 0, :, :], pq3, QSCALE, eneg, ALU.mult, ALU.mult),
                     reads=[b_pq, b_eneg], writes=[b_qk3])
        def gla_v(t):
            xnT, b_xnT = t["xnT"], t["b_xnT"]
            vt, b_vt = vtok_r.next()
            t["vt"], t["b_vt"] = vt, b_vt
            for nb in range(2):
                pv, b_pv = PS["g"].next()
                for k in range(8):
                    mm(pv[:], xnT[:, k, :], wgB[:, k, 512 + nb * 512:512 + (nb + 1) * 512], k == 0, k == 7,
                       [b_xnT, b_wgB], [b_pv])
                S.op("act", lambda pv=pv, nb=nb: nc.scalar.copy(vt[:, nb * 512:(nb + 1) * 512], pv[:]),
                     reads=[b_pv], writes=[b_vt])

        def gla_kdtok(t):
            kdT, b_kdT = t["kdT"], t["b_kdT"]
            ptk, b_ptk = PS["g"].next()
            ptk3 = v3(ptk[:], 4)
            for h in range(4):
                mm(ptk3[:, h, :], kdT[:, h, :], ident[:], True, True, [b_kdT, b_ident], [b_ptk])
            kdtok, b_kdtok = kdtok_r.next()
            S.op("act", lambda: nc.scalar.copy(kdtok, ptk3), reads=[b_ptk], writes=[b_kdtok])
            t["kdtok"], t["b_kdtok"] = kdtok, b_kdtok

        def gla_state(t):
            kdtok, b_kdtok, vt, b_vt = t["kdtok"], t["b_kdtok"], t["vt"], t["b_vt"]
            dfac, b_dfac = t["dfac"], t["b_dfac"]
            for hb in range(2):
                pS, b_pS = PS["g"].next()
                pS3 = v3(pS[:], 2)
                for h2 in range(2):
                    h = hb * 2 + h2
                    mm(pS3[:, h2, :], kdtok[:, h, :], vt[:, h * 256:(h + 1) * 256], True, True,
                       [b_kdtok, b_vt], [b_pS])
                for h2 in range(2):
                    h = hb * 2 + h2
                    S.op("dve", lambda pS3=pS3, h2=h2, h=h: nc.vector.scalar_tensor_tensor(
                        Sst[:, h, :], Sst[:, h, :], dfac[:, h, 0:1], pS3[:, h2, :], ALU.mult, ALU.add),
                        reads=[b_Sh[h], b_dfac, b_pS], writes=[b_Sh[h]])

        def after_layer0():
            fence([b_win0, b_wgA] + [b_ for (_, b_) in qk3_r.items] + [b_ for (_, b_) in qp_ra])
            gwA = DmaGroup("wgA")
            for k in range(8):
                S.dma("pool", lambda k=k: nc.gpsimd.dma_start(out=wgA[:, k, 0:512], in_=gin_v[:, k, 0:512]),
                      writes=[b_wgA], group=gwA)
                S.dma("pool", lambda k=k: nc.gpsimd.dma_start(out=wgA[:, k, 512:1536], in_=gin_v[:, k, 2048:3072]),
                      writes=[b_wgA], group=gwA)
            fence([b_gw, b_t12] + [b_ for (_, b_) in sc_r.items])
            fence([b_wout0, b_wout1])
            gwO = DmaGroup("wout1")
            for k in range(8):
                S.dma("pool", lambda k=k: nc.gpsimd.dma_start(out=w_out1[:, k, :], in_=gow_v[:, k, :]),
                      writes=[b_wout1], group=gwO)
            cdma(mf[:], d_mf, b_mf, grp=glate_d["pool"])
            cdma(mb[:], d_mb, b_mb, grp=glate_d["pool"])
            cdma(fnwbc[:], d_fnw.partition_broadcast(128), b_fnwbc, "sp", grp=glate_d["sp"])

        def fold_hnw(ks):
            for k in ks:
                S.op("act", lambda k=k: nc.scalar.activation(w_out1[:, k, :], w_out1[:, k, :], AF.Copy,
                                                             scale=hnw2[:, (k % 2):(k % 2) + 1]),
                     reads=[b_wout1, b_hnw2], writes=[b_wout1])

        mf, b_mf = pcur, b_pcur
        mb, b_mb = pprev, b_pprev
        fnwbc, b_fnwbc = nwbc0, b_nwbc0

        t2 = {}

        def p2_norm_a(i):
            t2[i] = {}
            norm_a(t2[i], h1[:, i, :], b_h1[i], nwbc1, b_nwbc1)

        def p2_state(i):
            gla_state(t2[i])
            del t2[i]

        def before_prepass():
            fence(xt_all_bufs)
            S.op("pool", lambda: nc.gpsimd.memset(Sst[:], 0.0), writes=b_Sh)

        p2_stages = [(lambda i: gla_gk_a(t2[i]), 2), (lambda i: norm_t(t2[i]), 1), (p2_norm_a, 0),
                     (lambda i: gla_kv(t2[i], False), 3), (lambda i: gla_gk_b(t2[i]), 2),
                     (lambda i: gla_v(t2[i]), 3), (lambda i: norm_d(t2[i]), 0),
                     (lambda i: gla_kdtok(t2[i]), 4), (p2_state, 5),
                     (lambda i: gla_gk_c(t2[i], False), 2)]
        P2OFF = NT + 2
        run_multi([(p1_stages(), 0, NT + 1, 0), (p2_stages, 0, NT, P2OFF)],
                  {P2OFF: before_prepass, NT + 7: after_layer0})

        b_xin, b_xout = Buf("xin"), Buf("xout")
        def send_state():
            S.op("act", lambda: nc.scalar.activation(Sst[:], Sst[:], AF.Copy, scale=maskA[:, 0:1]),
                 reads=b_Sh + [b_maskA], writes=b_Sh)
            S.dma("pool", lambda: nc.gpsimd.dma_start(out=d_xin_t[:, :], in_=Sst[:].rearrange("p a b -> p (a b)")),
                  reads=b_Sh, writes=[b_xin], buf=b_xin)
            S.dma("pool", lambda: nc.gpsimd.collective_compute(
                "AllReduce", ALU.add, replica_groups=[[0, 1], [2, 3], [4, 5], [6, 7]],
                ins=[d_xin_t.ap().opt()], outs=[d_xout_t.ap().opt()]),
                reads=[b_xin], writes=[b_xout], buf=b_xout, inc=1)

        def recv_state():
            S.dma("pool", lambda: nc.gpsimd.dma_start(out=Sst[:].rearrange("p a b -> p (a b)"), in_=d_xout_t[:, :]),
                  reads=[b_xout], writes=b_Sh, buf=b_S)
            S.op("act", lambda: nc.scalar.activation(Sst[:], Sst[:], AF.Copy, scale=maskB[:, 0:1]),
                 reads=b_Sh + [b_maskB], writes=b_Sh)
            S.op("pool", lambda: nc.gpsimd.tensor_copy(Sbf[:], Sst[:]), reads=b_Sh, writes=[b_Sbf])

        PS["g"] = psg
        t3 = {}

        def p3_norm_a(i):
            t3[i] = {}
            norm_a(t3[i], h1[:, i, :], b_h1[i], nwbc1, b_nwbc1)
            if 1 <= i <= 4:
                fold_hnw([2 * (i - 1), 2 * (i - 1) + 1])
            if i == 0:
                send_state()

        def p3_gate(i):
            t = t3[i]
            xnT, b_xnT = t["xnT"], t["b_xnT"]
            sgt, b_sgt = sgt_r.next()
            t["sgt"], t["b_sgt"] = sgt, b_sgt
            for nb in range(2):
                pg, b_pg = PS["g"].next()
                for k in range(8):
                    mm(pg[:], xnT[:, k, :], wgA[:, k, 512 + nb * 512:512 + (nb + 1) * 512], k == 0, k == 7,
                       [b_xnT, b_wgA], [b_pg])
                S.op("act", lambda pg=pg, nb=nb: nc.scalar.activation(sgt[:, nb * 512:(nb + 1) * 512], pg[:], AF.Silu),
                     reads=[b_pg], writes=[b_sgt])

        def p3_scores(i):
            t = t3[i]
            if i == 0:
                recv_state()
            qk3, b_qk3, qpos, b_qpos = t["qk3"], t["b_qk3"], t["qpos"], t["b_qpos"]
            pf, b_pf = PS["g"].next()
            pb, b_pb = PS["g"].next()
            pf3, pb3 = v3(pf[:], 4), v3(pb[:], 4)
            for h in range(4):
                mm(pf3[:, h, :], qk3[:, 1, h, :], qpos[:, h, :], True, True, [b_qk3, b_qpos], [b_pf])
            for h in range(4):
                mm(pb3[:, h, :], qk3[:, 2, h, :], qk3[:, 0, h, :], True, True, [b_qk3], [b_pb])
            sc, b_sc = sc_r.next()
            t["sc"], t["b_sc"] = sc, b_sc
            S.op("dve", lambda: nc.vector.tensor_tensor(t12[:, 0, :, :], pf3, mf[:], ALU.mult),
                 reads=[b_pf, b_mf], writes=[b_t12])
            S.op("dve", lambda: nc.vector.tensor_tensor(t12[:, 1, :, :], pb3, mb[:], ALU.mult),
                 reads=[b_pb, b_mb], writes=[b_t12])
            S.op("pool", lambda: nc.gpsimd.tensor_tensor(sc, t12[:, 0, :, :], t12[:, 1, :, :], ALU.add),
                 reads=[b_t12], writes=[b_sc])
            gla_kdtok(t)

        def p3_o(i):
            t = t3[i]
            qpos, b_qpos, sc, b_sc, vt, b_vt = t["qpos"], t["b_qpos"], t["sc"], t["b_sc"], t["vt"], t["b_vt"]
            sgt, b_sgt = t["sgt"], t["b_sgt"]
            stt, b_st = stats.next()
            y, b_y = y_r.next()
            pos = []
            for hb in range(2):
                po, b_po = pso.next()
                po3 = v3(po[:], 2)
                for h2 in range(2):
                    h = hb * 2 + h2
                    mm(po3[:, h2, :], sc[:, h, :], vt[:, h * 256:(h + 1) * 256], True, False,
                       [b_sc, b_vt], [b_po])
                    mm(po3[:, h2, :], qpos[:, h, :], Sbf[:, h, :], False, True, [b_qpos, b_Sbf], [b_po])
                for h2 in range(2):
                    h = hb * 2 + h2
                    S.op("act", lambda po3=po3, h2=h2, h=h: nc.scalar.activation(
                        junk[:, 0:256], po3[:, h2, :], AF.Square, accum_out=stt[:, h:h + 1]),
                        reads=[b_po], writes=[b_st])
                pos.append((po3, b_po))
            gla_state(t)
            S.op("pool", lambda: nc.gpsimd.tensor_copy(Sbf[:], Sst[:]), reads=b_Sh, writes=[b_Sbf])
            S.op("act", lambda: nc.scalar.activation(stt[:, 4:8], stt[:, 0:4], AF.Ln, bias=EPS, scale=1.0 / 256),
                 reads=[b_st], writes=[b_st])
            S.op("act", lambda: nc.scalar.activation(stt[:, 0:4], stt[:, 4:8], AF.Exp, scale=-0.5),
                 reads=[b_st], writes=[b_st])
            for hb in range(2):
                po3, b_po = pos[hb]
                for h2 in range(2):
                    h = hb * 2 + h2
                    S.op("dve", lambda po3=po3, h2=h2, h=h: nc.vector.scalar_tensor_tensor(
                        y[:, h * 256:(h + 1) * 256], po3[:, h2, :], stt[:, h:h + 1], sgt[:, h * 256:(h + 1) * 256],
                        ALU.mult, ALU.mult),
                        reads=[b_po, b_st, b_sgt], writes=[b_y])
            t["y"], t["b_y"] = y, b_y

        def p3_yT(i):
            t = t3[i]
            y, b_y = t["y"], t["b_y"]
            yT, b_yT = yT_r.next()
            for hb in range(2):
                pT, b_pT = PS["g"].next()
                pT3 = v3(pT[:], 4)
                for c4 in range(4):
                    c = hb * 4 + c4
                    mm(pT3[:, c4, :], y[:, c * 128:(c + 1) * 128], ident[:], True, True, [b_y, b_ident], [b_pT])
                S.op("dve", lambda pT3=pT3, hb=hb: nc.vector.tensor_copy(yT[:, hb * 4:(hb + 1) * 4, :], pT3),
                     reads=[b_pT], writes=[b_yT])
            t["yT"], t["b_yT"] = yT, b_yT

        def p3_out(i):
            t = t3[i]
            yT, b_yT = t["yT"], t["b_yT"]
            h2, b_h2 = h1[:, i, :], b_h1[i]
            for nb in range(2):
                po, b_po = PS["g"].next()
                for k in range(8):
                    mm(po[:], yT[:, k, :], w_out1[:, k, nb * 512:(nb + 1) * 512], k == 0, k == 7,
                       [b_yT, b_wout1], [b_po])
                S.op("dve", lambda po=po, nb=nb: nc.vector.tensor_tensor(
                    h2[:, nb * 512:(nb + 1) * 512], po[:], h2[:, nb * 512:(nb + 1) * 512], ALU.add),
                    reads=[b_po, b_h2], writes=[b_h2])
            stt, b_st = stats.next()
            S.op("act", lambda: nc.scalar.activation(junk[:], h2, AF.Square, accum_out=stt[:, 0:1]),
                 reads=[b_h2], writes=[b_st])
            S.op("act", lambda: nc.scalar.activation(stt[:, 1:2], stt[:, 0:1], AF.Ln, bias=EPS, scale=1.0 / D),
                 reads=[b_st], writes=[b_st])
            S.op("act", lambda: nc.scalar.activation(stt[:, 2:3], stt[:, 1:2], AF.Exp, scale=-0.5),
                 reads=[b_st], writes=[b_st])
            S.op("act", lambda: nc.scalar.activation(h2, h2, AF.Copy, scale=stt[:, 2:3]),
                 reads=[b_h2, b_st], writes=[b_h2])
            if i == NT - 1:
                S.op("dve", lambda: nc.vector.tensor_tensor(h2, h2, fnwbc[:], ALU.mult),
                     reads=[b_h2, b_fnwbc], writes=[b_h2])
            else:
                S.op("pool", lambda: nc.gpsimd.tensor_tensor(h2, h2, fnwbc[:], ALU.mult),
                     reads=[b_h2, b_fnwbc], writes=[b_h2])
            S.dma("sp", lambda: nc.sync.dma_start(out=d_out[i * 128:(i + 1) * 128, :], in_=h2),
                  reads=[b_h2], buf=b_h2)
            del t3[i]

        run_pipe([(lambda i: gla_kv(t3[i], True), 3), (lambda i: gla_gk_a(t3[i]), 2),
                  (lambda i: norm_t(t3[i]), 1), (p3_norm_a, 0),
                  (lambda i: gla_v(t3[i]), 3), (lambda i: gla_gk_b(t3[i]), 2), (p3_gate, 3),
                  (p3_scores, 4), (lambda i: norm_d(t3[i]), 0), (p3_o, 5),
                  (lambda i: gla_gk_c(t3[i], True), 2), (p3_out, 7), (p3_yT, 6)], 0, NT)

        S.lower()
    return nc


def _consts():
    c = {}
    c["ident"] = np.eye(128, dtype=np.float32)
    c["ones"] = np.ones((1, 128), np.float32)
    s = np.arange(128)[:, None]
    t = np.arange(128)[None, :]
    c["tri"] = np.where(s <= t, -1.0 / 16, 0.0).astype(np.float32)
    c["trirev"] = np.where(s > t, -1.0 / 16, 0.0).astype(np.float32)
    c["tot"] = np.full((128, 2), -1.0 / 16, np.float32)
    mf = (s <= t).astype(np.float32)
    mb = ((s > t) & ((s // 64) == (t // 64))).astype(np.float32)
    c["mf"] = np.ascontiguousarray(np.broadcast_to(mf[:, None, :], (128, 4, 128)))
    c["mb"] = np.ascontiguousarray(np.broadcast_to(mb[:, None, :], (128, 4, 128)))
    pcur = np.zeros((128, 4, 128), np.float32)
    pcurf = np.zeros((128, 4, 128), np.float32)
    pprev = np.zeros((128, 4, 128), np.float32)
    for g, w in enumerate((2, 4, 8, 16)):
        dist = t - s
        band = (dist >= 0) & (dist < w)
        pcur[:, g, :] = np.where(band, 1.0 / w, 0.0) - (dist == 0)
        cnt = np.minimum(t + 1, w).astype(np.float32)
        pcurf[:, g, :] = np.where(band, 1.0 / cnt, 0.0) - (dist == 0)
        dprev = t + 128 - s
        pprev[:, g, :] = np.where((dprev < w) & (s >= 64), 1.0 / w, 0.0)
    c["pcur"], c["pcurf"], c["pprev"] = pcur, pcurf, pprev
    return c


_NC = {}


def _get(mode):
    if mode not in _NC:
        _NC[mode] = build(mode)
    return _NC[mode]


def _maps_fused(x, norm_w, pool_in_w, pool_group_w, pool_group_b, pool_scale, pool_out_w,
                gla_in_w, gla_gk_w, gla_gk_b, gla_head_norm_w, gla_out_w, final_norm_w):
    f = lambda a: np.ascontiguousarray(np.asarray(a, dtype=np.float32))
    x = f(x)
    c = _consts()
    common = dict(ident=c["ident"], norm_w=f(norm_w), gla_in_w=f(gla_in_w[0]), gla_gk_w=f(gla_gk_w[0]),
                  gla_gk_b=f(gla_gk_b[0]).reshape(1, 512), ones=c["ones"], tri=c["tri"], trirev=c["trirev"],
                  tot=c["tot"], pool_in_w=f(pool_in_w[0]), pool_group_w=f(pool_group_w[0]),
                  pool_group_b=f(pool_group_b[0]).reshape(D), pool_scale=f(pool_scale[0]).reshape(D),
                  pool_out_w=f(pool_out_w[0]), pcur=c["pcur"], pprev=c["pprev"],
                  hnw2=np.ascontiguousarray(f(gla_head_norm_w[0]).reshape(2, 128).T),
                  gla_out_w=f(gla_out_w[0]),
                  final_norm_w=f(final_norm_w).reshape(D), mf=c["mf"], mb=c["mb"])
    maps = []
    for core in range(8):
        b, half = core // 2, core % 2
        own = x[b, half * TOK:(half + 1) * TOK]
        halo = np.zeros((128, D), np.float32) if half == 0 else x[b, TOK - 128:TOK]
        m = dict(common)
        m.update(x=np.ascontiguousarray(np.concatenate([halo, own], 0)),
                 pcurf=c["pcurf"] if half == 0 else c["pcur"],
                 mask_a=np.full((128, 1), 1.0 if half == 0 else 0.0, np.float32),
                 mask_b=np.full((128, 1), 1.0 if half == 1 else 0.0, np.float32))
        maps.append(m)
    return maps


def kernel(x, norm_w, pool_in_w, pool_group_w, pool_group_b, pool_scale, pool_out_w,
           gla_in_w, gla_gk_w, gla_gk_b, gla_head_norm_w, gla_out_w, final_norm_w):
    maps = _maps_fused(x, norm_w, pool_in_w, pool_group_w, pool_group_b, pool_scale, pool_out_w,
                       gla_in_w, gla_gk_w, gla_gk_b, gla_head_norm_w, gla_out_w, final_norm_w)
    res = run_bass_kernel_spmd(_get("F"), maps, core_ids=list(range(8))).results
    out = np.empty((4, 2 * TOK, D), np.float32)
    for core in range(8):
        b, half = core // 2, core % 2
        out[b, half * TOK:(half + 1) * TOK] = res[core]["out"]
    return out
```
